# Optimizing a Trainium2 kernel written in Bass

```python
import jax, jax.numpy as jnp
from jax import lax
import numpy as np


D_MODEL = 1024
BATCH = 16
SEQ = 2048
DEPTH = 1

HEAD_DIM = 64
NSA_HEADS = 8
NSA_KV = 2
NSA_GROUP = NSA_HEADS // NSA_KV
CMP_LEN = 32
CMP_STRIDE = 16
CMP_HIDDEN = 256
SEL_BLOCK = 64
SEL_TOPN = 8
SEL_Q_BLOCK = 64
NSA_WINDOW = 512
SWA_HEADS = 8
SWA_KV = 2
SWA_GROUP = SWA_HEADS // SWA_KV
SWA_WINDOW = 128
Q_BLOCK = 128
ROPE_THETA = 10000.0
LN_EPS = 1e-5
NEG_INF = -1e30
FORCE_SCORE = 1e9
NSA_WIDTH = NSA_HEADS * HEAD_DIM
SWA_WIDTH = SWA_HEADS * HEAD_DIM
NSA_KV_WIDTH = NSA_KV * HEAD_DIM
SWA_KV_WIDTH = SWA_KV * HEAD_DIM
DEEPNORM_ALPHA = (2 * DEPTH) ** 0.25
DEEPNORM_BETA = (8 * DEPTH) ** -0.25
IN_SPLITS = [NSA_WIDTH,
             2 * NSA_KV_WIDTH,
             2 * NSA_KV_WIDTH,
             2 * NSA_KV_WIDTH,
             3 * NSA_HEADS,
             NSA_WIDTH,
             SWA_WIDTH,
             2 * SWA_KV_WIDTH,
             SWA_WIDTH,
             2 * D_MODEL]
IN_WIDTH = sum(IN_SPLITS)

kernel_name = 'nsa_swa_sink_griffin_merge_deepnorm_adaln'


def _layer_norm(x):
    xf = x.astype(jnp.float32)
    mu = jnp.mean(xf, axis=-1, keepdims=True)
    var = jnp.mean(jnp.square(xf - mu), axis=-1, keepdims=True)
    return ((xf - mu) * lax.rsqrt(var + LN_EPS)).astype(x.dtype)


def _rope(t, positions):
    half = HEAD_DIM // 2
    inv_freq = ROPE_THETA ** (-jnp.arange(half, dtype=jnp.float32) / half)
    ang = positions.astype(jnp.float32)[..., None] * inv_freq
    shp = ang.shape[:2] + (1,) * (t.ndim - 3) + (half,)
    cos = jnp.cos(ang).reshape(shp)
    sin = jnp.sin(ang).reshape(shp)
    tf = t.astype(jnp.float32)
    t1, t2 = tf[..., :half], tf[..., half:]
    return jnp.concatenate([t1 * cos - t2 * sin, t2 * cos + t1 * sin], axis=-1).astype(t.dtype)


def _split_columns(h):
    offs = [int(o) for o in np.cumsum([0] + IN_SPLITS)]
    return [h[..., offs[i]:offs[i + 1]] for i in range(len(IN_SPLITS))]


def _kv(t, n_kv, positions):
    B, S = t.shape[:2]
    k, v = jnp.split(t, 2, axis=-1)
    k = _rope(k.reshape(B, S, n_kv, HEAD_DIM), positions)
    return k, v.reshape(B, S, n_kv, HEAD_DIM)


def _banded_attention(qg, k, v, window, sinks=None):
    B, S, G, hpg, dh = qg.shape
    n_prev = -(-(window - 1) // Q_BLOCK)
    pad = n_prev * Q_BLOCK
    span = pad + Q_BLOCK
    kp = jnp.pad(k, ((0, 0), (pad, 0), (0, 0), (0, 0)))
    vp = jnp.pad(v, ((0, 0), (pad, 0), (0, 0), (0, 0)))
    rel = np.arange(Q_BLOCK)[:, None] + pad - np.arange(span)[None, :]
    in_band = jnp.asarray((rel >= 0) & (rel < window))
    scale = dh ** -0.5

    def block(i):
        start = i * Q_BLOCK
        qb = lax.dynamic_slice_in_dim(qg, start, Q_BLOCK, axis=1)
        kb = lax.dynamic_slice_in_dim(kp, start, span, axis=1)
        vb = lax.dynamic_slice_in_dim(vp, start, span, axis=1)
        mask = in_band & ((start - pad + jnp.arange(span)) >= 0)[None, :]
        s = jnp.einsum('bqghd,bkgd->bghqk', qb, kb).astype(jnp.float32) * scale
        s = jnp.where(mask, s, NEG_INF)
        if sinks is None:
            p = jax.nn.softmax(s, axis=-1)
        else:
            sink = jnp.broadcast_to(sinks.astype(jnp.float32).reshape(1, G, hpg, 1, 1), s.shape[:-1] + (1,))
            p = jax.nn.softmax(jnp.concatenate([s, sink], axis=-1), axis=-1)[..., :-1]
        return jnp.einsum('bghqk,bkgd->bqghd', p.astype(vb.dtype), vb)

    out = lax.map(block, jnp.arange(S // Q_BLOCK))
    return out.transpose(1, 0, 2, 3, 4, 5).reshape(B, S, G, hpg, dh)


def _compress(kv, pos_emb, w1, w2):
    S = kv.shape[1]
    n_cmp = (S - CMP_LEN) // CMP_STRIDE + 1
    idx = np.arange(n_cmp)[:, None] * CMP_STRIDE + np.arange(CMP_LEN)[None, :]
    blocks = kv[:, idx] + pos_emb[:, None, :]
    hid = jax.nn.silu(jnp.einsum('bclgd,ldf->bcgf', blocks, w1.reshape(CMP_LEN, HEAD_DIM, CMP_HIDDEN)))
    return jnp.einsum('bcgf,fd->bcgd', hid, w2)


def _compressed_attention(qg, kc, vc):
    S, n_cmp = qg.shape[1], kc.shape[1]
    visible = jnp.asarray((np.arange(n_cmp) * CMP_STRIDE + CMP_LEN - 1)[None, :] <= np.arange(S)[:, None])
    s = jnp.einsum('bsghd,bcgd->bghsc', qg, kc).astype(jnp.float32) * (HEAD_DIM ** -0.5)
    s = jnp.where(visible, s, NEG_INF)
    p = jax.nn.softmax(s, axis=-1) * visible
    o = jnp.einsum('bghsc,bcgd->bsghd', p.astype(vc.dtype), vc)
    return o, p


def _select_blocks(p_cmp, S):
    n_cmp = p_cmp.shape[-1]
    n_blk = S // SEL_BLOCK
    c0 = np.arange(n_cmp) * CMP_STRIDE
    j0 = np.arange(n_blk) * SEL_BLOCK
    overlap = (c0[:, None] < j0[None, :] + SEL_BLOCK) & (c0[:, None] + CMP_LEN > j0[None, :])
    imp = jnp.einsum('bghsc,cj->bgsj', p_cmp, jnp.asarray(overlap, jnp.float32))
    cur = np.arange(S) // SEL_BLOCK
    j = np.arange(n_blk)
    causal = j[None, :] <= cur[:, None]
    forced = (j[None, :] == 0) | (j[None, :] == cur[:, None]) | (j[None, :] == cur[:, None] - 1)
    score = jnp.where(jnp.asarray(forced), FORCE_SCORE, imp)
    score = jnp.where(jnp.asarray(causal), score, NEG_INF)
    _, idx = lax.top_k(score, min(SEL_TOPN, n_blk))
    return idx


def _selected_attention(qg, k, v, idx):
    B, S, G, hpg, dh = qg.shape
    n_blk = S // SEL_BLOCK
    n_top = idx.shape[-1]
    kb = k.reshape(B, n_blk, SEL_BLOCK, G, dh).transpose(0, 3, 1, 2, 4)
    vb = v.reshape(B, n_blk, SEL_BLOCK, G, dh).transpose(0, 3, 1, 2, 4)
    bi = jnp.arange(B)[:, None, None, None]
    gi = jnp.arange(G)[None, :, None, None]
    scale = dh ** -0.5

    def block(i):
        start = i * SEL_Q_BLOCK
        qb = lax.dynamic_slice_in_dim(qg, start, SEL_Q_BLOCK, axis=1)
        ib = lax.dynamic_slice_in_dim(idx, start, SEL_Q_BLOCK, axis=2)
        kg = kb[bi, gi, ib]
        vg = vb[bi, gi, ib].reshape(B, G, SEL_Q_BLOCK, n_top * SEL_BLOCK, dh)
        key_pos = ib[..., None] * SEL_BLOCK + jnp.arange(SEL_BLOCK)
        q_pos = start + jnp.arange(SEL_Q_BLOCK)
        mask = (key_pos <= q_pos[:, None, None]).reshape(B, G, 1, SEL_Q_BLOCK, n_top * SEL_BLOCK)
        s = jnp.einsum('bqghd,bgqnkd->bghqnk', qb, kg).astype(jnp.float32) * scale
        s = jnp.where(mask, s.reshape(B, G, hpg, SEL_Q_BLOCK, n_top * SEL_BLOCK), NEG_INF)
        p = jax.nn.softmax(s, axis=-1).astype(vg.dtype)
        return jnp.einsum('bghqm,bgqmd->bqghd', p, vg)

    out = lax.map(block, jnp.arange(S // SEL_Q_BLOCK))
    return out.transpose(1, 0, 2, 3, 4, 5).reshape(B, S, G, hpg, dh)


def setup_inputs(seed: int = 0) -> dict:
    key = jax.random.key(seed)
    ks = jax.random.split(key, 20)
    nrm = lambda k, shape, s: jax.random.normal(k, shape, jnp.float32) * s
    L = DEPTH
    x = jax.random.normal(ks[0], (BATCH, SEQ, D_MODEL), jnp.float32)
    c = jax.random.normal(ks[1], (BATCH, D_MODEL), jnp.float32)
    positions = (jnp.arange(SEQ, dtype=jnp.int32)[None, :]
                 + jax.random.randint(ks[2], (BATCH, 1), 0, 1024, dtype=jnp.int32))
    return {
        'x': x,
        'c': c,
        'positions': positions,
        'w_ada': nrm(ks[3], (L, D_MODEL, 3 * D_MODEL), 0.5 * D_MODEL ** -0.5),
        'b_ada': nrm(ks[4], (L, 3 * D_MODEL), 0.01),
        'w_in': nrm(ks[5], (L, D_MODEL, IN_WIDTH), D_MODEL ** -0.5),
        'cmp_pos_k': nrm(ks[6], (L, CMP_LEN, HEAD_DIM), 0.02),
        'cmp_w1_k': nrm(ks[7], (L, CMP_LEN * HEAD_DIM, CMP_HIDDEN), (CMP_LEN * HEAD_DIM) ** -0.5),
        'cmp_w2_k': nrm(ks[8], (L, CMP_HIDDEN, HEAD_DIM), CMP_HIDDEN ** -0.5),
        'cmp_pos_v': nrm(ks[9], (L, CMP_LEN, HEAD_DIM), 0.02),
        'cmp_w1_v': nrm(ks[10], (L, CMP_LEN * HEAD_DIM, CMP_HIDDEN), (CMP_LEN * HEAD_DIM) ** -0.5),
        'cmp_w2_v': nrm(ks[11], (L, CMP_HIDDEN, HEAD_DIM), CMP_HIDDEN ** -0.5),
        'sinks': nrm(ks[12], (L, SWA_HEADS), 0.5),
        'w_up_a': nrm(ks[13], (L, NSA_WIDTH, D_MODEL), DEEPNORM_BETA * NSA_WIDTH ** -0.5),
        'w_up_b': nrm(ks[14], (L, SWA_WIDTH, D_MODEL), DEEPNORM_BETA * SWA_WIDTH ** -0.5),
        'w_out': nrm(ks[15], (L, D_MODEL, D_MODEL), DEEPNORM_BETA * D_MODEL ** -0.5),
        'ln_g': 1.0 + nrm(ks[16], (L, D_MODEL), 0.01),
        'ln_b': nrm(ks[17], (L, D_MODEL), 0.01),
    }


def reference(x, c, positions, w_ada, b_ada, w_in, cmp_pos_k, cmp_w1_k, cmp_w2_k,
              cmp_pos_v, cmp_w1_v, cmp_w2_v, sinks, w_up_a, w_up_b, w_out, ln_g, ln_b):
    B, S, _ = x.shape
    for l in range(DEPTH):
        mod = jax.nn.silu(c) @ w_ada[l] + b_ada[l]
        shift, scale, gate = jnp.split(mod, 3, axis=-1)
        u = _layer_norm(x) * (1.0 + scale[:, None, :]) + shift[:, None, :]
        (q_a, kv_cmp, kv_sel, kv_win, g_nsa, z_a,
         q_b, kv_b, z_b, g_merge) = _split_columns(u @ w_in[l])

        qa = _rope(q_a.reshape(B, S, NSA_KV, NSA_GROUP, HEAD_DIM), positions)
        k_c, v_c = _kv(kv_cmp, NSA_KV, positions)
        k_s, v_s = _kv(kv_sel, NSA_KV, positions)
        k_w, v_w = _kv(kv_win, NSA_KV, positions)
        kc = _compress(k_c, cmp_pos_k[l], cmp_w1_k[l], cmp_w2_k[l])
        vc = _compress(v_c, cmp_pos_v[l], cmp_w1_v[l], cmp_w2_v[l])
        o_cmp, p_cmp = _compressed_attention(qa, kc, vc)
        sel_idx = _select_blocks(p_cmp, S)
        o_sel = _selected_attention(qa, k_s, v_s, sel_idx)
        o_win = _banded_attention(qa, k_w, v_w, NSA_WINDOW)
        gts = jax.nn.sigmoid(g_nsa.reshape(B, S, NSA_KV, NSA_GROUP, 3))
        o_a = (gts[..., 0:1] * o_cmp + gts[..., 1:2] * o_sel + gts[..., 2:3] * o_win).reshape(B, S, NSA_WIDTH)
        y_a = o_a * jax.nn.silu(z_a)

        qb = _rope(q_b.reshape(B, S, SWA_KV, SWA_GROUP, HEAD_DIM), positions)
        k_b, v_b = _kv(kv_b, SWA_KV, positions)
        o_b = _banded_attention(qb, k_b, v_b, SWA_WINDOW, sinks[l]).reshape(B, S, SWA_WIDTH)
        y_b = o_b * jax.nn.silu(z_b)

        gm_a, gm_b = jnp.split(g_merge, 2, axis=-1)
        merged = jax.nn.sigmoid(gm_a) * (y_a @ w_up_a[l]) + jax.nn.sigmoid(gm_b) * (y_b @ w_up_b[l])
        out = merged @ w_out[l]
        x = _layer_norm(DEEPNORM_ALPHA * x + gate[:, None, :] * out) * ln_g[l] + ln_b[l]
    return x
```

```python
import numpy as np
from contextlib import ExitStack
import concourse.bass as bass
import concourse.mybir as mybir
from concourse.bass_utils import run_bass_kernel_spmd

F32 = mybir.dt.float32
BF16 = mybir.dt.bfloat16
I32 = mybir.dt.int32
AF = mybir.ActivationFunctionType
ALU = mybir.AluOpType
AX = mybir.AxisListType


class Dep:
    __slots__ = ("w", "readers", "name", "bank", "last")

    def __init__(self, name="", bank=None):
        self.w = None
        self.readers = {}
        self.name = name
        self.bank = bank
        self.last = None


class _Rec:
    def __init__(self):
        self.calls = []

    def __getattr__(self, name):
        def f(*a, **k):
            self.calls.append((name, a, k))
            return None
        return f


class Sched:
    ENG = ("pe", "act", "dve", "pool", "sp")

    def __init__(self, nc, stack, n_dma_sems=8):
        self.nc = nc
        self.stack = stack
        self.eng_obj = {"pe": nc.tensor, "act": nc.scalar, "dve": nc.vector,
                        "pool": nc.gpsimd, "sp": nc.sync}
        self.sems = {}
        for e in self.ENG:
            self.sems[e] = stack.enter_context(nc.semaphore("s_" + e))
        self.count = {e: 0 for e in self.ENG}
        self.prog = {e: [] for e in self.ENG}
        self.seen = {e: {} for e in self.ENG}
        self.dma_sems = {}
        self.dma_uses = {}
        self.dma_rr = {}
        for q in ("sp", "pool", "act"):
            lst = []
            for i in range(n_dma_sems):
                k = "d_%s%d" % (q, i)
                self.sems[k] = stack.enter_context(nc.semaphore(k))
                self.dma_uses[k] = 0
                lst.append(k)
            self.dma_sems[q] = lst
            self.dma_rr[q] = 0

    def _collect(self, eng, reads, writes, extra=()):
        waits = {}

        def add(tok):
            if tok is None:
                return
            k, v = tok
            if waits.get(k, 0) < v:
                waits[k] = v
        for d in reads:
            add(d.w)
        for d in writes:
            add(d.w)
            for k, v in d.readers.items():
                add((k, v))
        for t in extra:
            add(t)
        for d in list(reads) + list(writes):
            bk = d.bank
            if bk is not None and bk.last is not None and bk.last[0] != eng:
                add(bk.last)
        out = []
        seen = self.seen[eng]
        for k, v in waits.items():
            if k == eng:
                if eng == "pe" or self.count[eng] - v >= 3:
                    continue
            if seen.get(k, 0) >= v:
                continue
            seen[k] = v
            out.append((k, v))
        return out

    def _commit(self, tok, reads, writes):
        k, v = tok
        for d in reads:
            if d.readers.get(k, 0) < v:
                d.readers[k] = v
        for d in writes:
            d.w = tok
            d.readers = {}
        for d in list(reads) + list(writes):
            if d.bank is not None:
                d.bank.last = tok

    muted = False

    def op(self, eng, fn, reads=(), writes=()):
        if self.muted:
            return None
        waits = self._collect(eng, reads, writes)
        self.count[eng] += 1
        tok = (eng, self.count[eng])
        rec = _Rec()
        fn(rec)
        assert rec.calls
        self.prog[eng].append((waits, rec.calls, (eng, 1)))
        self._commit(tok, reads, writes)
        return tok

    def dma(self, q, fn, reads=(), writes=()):
        if self.muted:
            return None
        lst = self.dma_sems[q]
        k = lst[self.dma_rr[q] % len(lst)]
        self.dma_rr[q] += 1
        prev = (k, 16 * self.dma_uses[k]) if self.dma_uses[k] else None
        waits = self._collect(q, reads, writes, extra=(prev,) if prev else ())
        self.dma_uses[k] += 1
        tok = (k, 16 * self.dma_uses[k])
        rec = _Rec()
        fn(rec)
        assert len(rec.calls) == 1
        self.prog[q].append((waits, rec.calls, (k, 16)))
        self._commit(tok, reads, writes)
        return tok

    def finish(self):
        finals = []
        for e in self.ENG:
            if self.count[e]:
                finals.append((e, self.count[e]))
        for k, u in self.dma_uses.items():
            if u:
                finals.append((k, 16 * u))
        self.prog["sp"].append((finals, None, None))

    def emit(self, block):
        sems = self.sems

        def mk(e):
            prog = self.prog[e]

            def body(eng):
                for waits, fn, inc in prog:
                    for k, v in waits:
                        eng.wait_ge(sems[k], v)
                    if fn is not None:
                        ins = None
                        for name, a, k in fn:
                            ins = getattr(eng, name)(*a, **k)
                        ins.then_inc(sems[inc[0]], inc[1])
            return body
        block.tensor(mk("pe"))
        block.scalar(mk("act"))
        block.vector(mk("dve"))
        block.gpsimd(mk("pool"))
        block.sync(mk("sp"))


D = 1024
S = 2048
NB = 2
NT = S // 128
NST = NT // 4
LN_EPS = 1e-5
ALPHA = (2 * 1) ** 0.25
NEGM = 32768.0
PW = 2072
QW = 3072
MAGIC = 12582912.0
TWO_PI = 6.283185307179586
C1 = 6.28125
C2 = TWO_PI - C1
PI_LO = 3.1415925


def _mkap(base, off, dims):
    return bass.AP(base.tensor, base.offset + off, [list(base.ap[0])] + [list(d) for d in dims])


class Buf:
    def __init__(self, t, name, psum=False):
        self.t = t
        self.name = name
        self.deps = {}
        self.bank = Dep(name + "/bank") if psum else None

    def __getitem__(self, k):
        return self.t[k]

    def d(self, key=None):
        if key not in self.deps:
            self.deps[key] = Dep("%s/%s" % (self.name, key), bank=self.bank)
        return self.deps[key]


class _Stop(Exception):
    pass


def build_program(debug=None, stop=None):
    nc = bass.Bass("TRN2", target_bir_lowering=False)
    dbg_outs = {}

    muted = [False]

    def cp(name):
        if stop is not None and stop == name:
            muted[0] = True
            Sc_holder[0].muted = True
    Sc_holder = [None]

    def din(name, shape, dt=F32):
        return nc.dram_tensor(name, list(shape), dt, kind="ExternalInput").ap()

    x_d = din("x", [NB * S, D])
    out_d = nc.dram_tensor("out", [NB * S, D], F32, kind="ExternalOutput").ap()
    cT_d = din("cT", [128, 8, NB])
    pos_d = din("posT", [128, NB, NT], I32)
    wada_d = din("w_ada", [D, 3 * D])
    badaT_d = din("b_adaT", [128, 24])
    bgate_d = din("b_gate", [1, D])
    winP_d = din("w_inP", [D, PW])
    winQ_d = din("w_inQ", [D, QW])
    w1k_d = din("w1k", [128, 16, 256])
    w1v_d = din("w1v", [128, 16, 256])
    posk_d = din("posk", [128, 16])
    posv_d = din("posv", [128, 16])
    w2k_d = din("w2k", [128, 2, 128])
    w2v_d = din("w2v", [128, 2, 64])
    sinks_d = din("sinks", [1, 8])
    wupa_d = din("w_up_a", [512, D])
    wupb_d = din("w_up_b", [512, D])
    wout_d = din("w_out", [D, D])
    lng_d = din("ln_g", [1, D])
    lnb_d = din("ln_b", [1, D])
    ident_d = din("c_ident", [128, 128])
    maskc_d = din("c_maskc", [128, 128])
    maske_d = din("c_maske", [128, 128])
    cmpm_d = din("c_cmpm", [128, NT, 128])
    wst_d = din("c_wst", [32, S])
    fc_d = din("c_fc", [128, NT, 2, 32])
    vcinit_d = din("c_vcinit", [128, 2, 97])
    invf_d = din("c_invf", [128, 32])

    with ExitStack() as gs:
        Sc = Sched(nc, gs)
        Sc_holder[0] = Sc

        def sbuf(st, name, shape, dt):
            return Buf(st.enter_context(nc.sbuf_tensor("sb_" + name, list(shape), dt)), name)

        def psum(st, name, shape, dt):
            return Buf(st.enter_context(nc.psum_tensor("ps_" + name, list(shape), dt)), name, psum=True)

        def dump(name, buf_ap, dep, shape, dt=F32):
            if debug is None or name not in debug:
                return
            o = nc.dram_tensor("dbg_" + name, list(shape), dt, kind="ExternalOutput").ap()
            dbg_outs[name] = o
            Sc.dma("sp", lambda e: e.dma_start(out=o, in_=buf_ap), reads=[dep])

        def barrier():
            if Sc.muted:
                return
            toks = [(e, Sc.count[e]) for e in Sc.ENG if Sc.count[e]]
            toks += [(k, 16 * u) for k, u in Sc.dma_uses.items() if u]
            for e in Sc.ENG:
                w = []
                for k, v in toks:
                    if k == e:
                        continue
                    if Sc.seen[e].get(k, 0) < v:
                        Sc.seen[e][k] = v
                        w.append((k, v))
                if w:
                    Sc.prog[e].append((w, None, None))

        def body():
            pj = [psum(gs, "pj%d" % i, [128, 512], F32) for i in range(2)]
            pst = [psum(gs, "pst%d" % i, [128, 512], F32) for i in range(2)]
            po = [psum(gs, "po%d" % i, [128, 512], F32) for i in range(2)]
            ptp = psum(gs, "ptp", [128, 1024], BF16)
            pmisc = psum(gs, "pmisc", [128, 512], F32)

            OA = sbuf(gs, "OA", [128, NB * NT, 1024], BF16)
            gateT = sbuf(gs, "gateT", [128, 8, NB], F32)
            onepT = sbuf(gs, "onepT", [128, 8, NB], F32)
            shiftT = sbuf(gs, "shiftT", [128, 8, NB], F32)
            ident = sbuf(gs, "ident", [128, 128], BF16)
            identf = sbuf(gs, "identf", [128, 128], F32)
            epsb = sbuf(gs, "epsb", [128, 1], F32)

            Sc.dma("pool", lambda e: e.dma_start(out=ident[:], in_=ident_d), writes=[ident.d()])
            Sc.dma("sp", lambda e: e.dma_start(out=identf[:], in_=ident_d), writes=[identf.d()])
            Sc.op("dve", lambda e: e.memset(epsb[:], LN_EPS), writes=[epsb.d()])

            def ln_stats(xt, xdep, st6, mv, rstd, nmr, tagdeps):
                sd, md, rd, nd = tagdeps
                Sc.op("dve", lambda e: e.bn_stats(out=st6[:, 0, :], in_=xt[:, 0:512]), reads=[xdep], writes=[sd])
                Sc.op("dve", lambda e: e.bn_stats(out=st6[:, 1, :], in_=xt[:, 512:1024]), reads=[xdep], writes=[sd])
                Sc.op("dve", lambda e: e.bn_aggr(out=mv, in_=st6[:].rearrange("p a b -> p (a b)")), reads=[sd], writes=[md])
                Sc.op("act", lambda e: e.activation(out=rstd, in_=mv[:, 1:2], func=AF.Ln, bias=epsb[:], scale=1.0),
                      reads=[md, epsb.d()], writes=[rd])
                Sc.op("act", lambda e: e.activation(out=rstd, in_=rstd, func=AF.Exp, scale=-0.5), reads=[rd], writes=[rd])
                Sc.op("dve", lambda e: e.tensor_scalar(out=nmr, in0=mv[:, 0:1], scalar1=rstd, scalar2=-1.0,
                                                      op0=ALU.mult, op1=ALU.mult), reads=[md, rd], writes=[nd])

            def ln_mod_transpose(xb, xdep, b, uT_ap_fn, uT_dep, small, smalldeps):
                st6, mv, rstd, nmr = small
                ln_stats(xb, xdep, st6, mv, rstd, nmr, smalldeps)
                sd, md, rd, nd = smalldeps
                Sc.op("dve", lambda e: e.tensor_scalar(out=xb[:, :], in0=xb[:, :], scalar1=rstd, scalar2=nmr,
                                                      op0=ALU.mult, op1=ALU.add), reads=[xdep, rd, nd], writes=[xdep])
                for half in range(2):
                    pb = pj[half]

                    def tr(e, half=half, pb=pb):
                        ins = None
                        for i in range(4):
                            kc = half * 4 + i
                            ins = e.transpose(out=pb[:, i * 128:(i + 1) * 128], in_=xb[:, kc * 128:(kc + 1) * 128],
                                              identity=identf[:])
                        return ins
                    Sc.op("pe", tr, reads=[xdep, identf.d()], writes=[pb.d()])
                    for i in range(4):
                        kc = half * 4 + i
                        Sc.op("act", lambda e, kc=kc, i=i, pb=pb: e.activation(
                            out=uT_ap_fn(kc), in_=pb[:, i * 128:(i + 1) * 128], func=AF.Identity,
                            scale=onepT[:, kc, b:b + 1], bias=shiftT[:, kc, b:b + 1]),
                            reads=[pb.d(), onepT.d(), shiftT.d()], writes=[uT_dep])

            with ExitStack() as ps_:
                winP = sbuf(ps_, "winP", [128, 8, PW], BF16)
                W1 = [sbuf(ps_, "W1k", [128, 16, 256], BF16), sbuf(ps_, "W1v", [128, 16, 256], BF16)]
                posr = [sbuf(ps_, "posk", [128, 16], BF16), sbuf(ps_, "posv", [128, 16], BF16)]
                W2k = sbuf(ps_, "W2k", [128, 2, 128], BF16)
                W2v = sbuf(ps_, "W2v", [128, 2, 64], BF16)
                biasT = sbuf(ps_, "biasT", [128, 2, 2], F32)
                nbiasT = sbuf(ps_, "nbiasT", [128, 2, 2], F32)
                KT = sbuf(ps_, "KT", [128, 2, S], BF16)
                KTs = sbuf(ps_, "KTs", [128, 2, S], BF16)
                VC = sbuf(ps_, "VC", [128, NT, 3, 2, 65], BF16)
                XcT = sbuf(ps_, "XcT", [128, 4, 528], BF16)
                kcT = sbuf(ps_, "kcT", [128, 128], BF16)
                VCext = sbuf(ps_, "VCext", [128, 2, 97], BF16)
                maskc = sbuf(ps_, "maskc", [128, 128], BF16)
                maske = sbuf(ps_, "maske", [128, 128], BF16)
                cmpm = sbuf(ps_, "cmpm", [128, NT, 128], BF16)
                fct = sbuf(ps_, "fct", [128, NT, 2, 32], BF16)
                invf = sbuf(ps_, "invf", [128, 32], F32)
                esink = sbuf(ps_, "esink", [128, 8], F32)
                cs2 = sbuf(ps_, "cs2", [128, NT, 32], F32)
                sn2 = sbuf(ps_, "sn2", [128, NT, 32], F32)

                with ExitStack() as p0:
                    wadab = [sbuf(p0, "wada%d" % i, [128, 8, 512], F32) for i in range(2)]
                    cTs = sbuf(p0, "cTs", [128, 8, NB], F32)
                    scT = sbuf(p0, "scT", [128, 8, NB], F32)
                    badaT = sbuf(p0, "badaT", [128, 24], F32)
                    modT = sbuf(p0, "modT", [128, 24, NB], F32)

                    Sc.dma("sp", lambda e: e.dma_start(out=cTs[:], in_=cT_d), writes=[cTs.d()])
                    Sc.dma("sp", lambda e: e.dma_start(out=badaT[:], in_=badaT_d), writes=[badaT.d()])
                    Sc.op("act", lambda e: e.activation(out=scT[:], in_=cTs[:], func=AF.Silu),
                          reads=[cTs.d()], writes=[scT.d()])
                    wada_v = wada_d.rearrange("(k p) n -> p k n", p=128)
                    pm = pmisc[:, 0:48].rearrange("p (j b) -> p j b", b=NB)
                    for blk in range(6):
                        wb_ = wadab[blk % 2]
                        Sc.dma("sp", lambda e: e.dma_start(out=wb_[:], in_=wada_v[:, :, blk * 512:(blk + 1) * 512]),
                               writes=[wb_.d()])

                        def mm_mod(e):
                            for jj in range(4):
                                j = blk * 4 + jj
                                for kc in range(8):
                                    e.matmul(pm[:, j, :], lhsT=wb_[:, kc, jj * 128:(jj + 1) * 128], rhs=scT[:, kc, :],
                                             start=(kc == 0), stop=(kc == 7))
                        Sc.op("pe", mm_mod, reads=[scT.d(), wb_.d()], writes=[pmisc.d()])
                    Sc.op("dve", lambda e: e.tensor_tensor(
                        out=modT[:], in0=pm, in1=badaT[:].unsqueeze(2).broadcast_to([128, 24, NB]), op=ALU.add),
                        reads=[pmisc.d(), badaT.d()], writes=[modT.d()])
                    Sc.op("dve", lambda e: e.tensor_copy(out=shiftT[:], in_=modT[:, 0:8, :]),
                          reads=[modT.d()], writes=[shiftT.d()])
                    Sc.op("dve", lambda e: e.tensor_scalar(out=onepT[:], in0=modT[:, 8:16, :], scalar1=1.0, scalar2=None,
                                                          op0=ALU.add), reads=[modT.d()], writes=[onepT.d()])
                    Sc.op("dve", lambda e: e.tensor_copy(out=gateT[:], in_=modT[:, 16:24, :]),
                          reads=[modT.d()], writes=[gateT.d()])
                    winP_v = winP_d.rearrange("(k p) n -> p k n", p=128)
                    for (c0, c1) in ((1024, 1536), (1536, 2072), (0, 512), (512, 1024)):
                        for kc in range(8):
                            Sc.dma("pool", lambda e: e.dma_start(
                                out=winP[:, kc, c0:c1], in_=winP_v[:, kc, c0:c1], max_dma_last_dim=2048),
                                writes=[winP.d((c0, kc))])
                    for i, (src, dst) in enumerate(((w1k_d, W1[0]), (w1v_d, W1[1]))):
                        for h in range(2):
                            Sc.dma("pool", lambda e: e.dma_start(
                                out=dst[:, h * 8:(h + 1) * 8, :], in_=src[:, h * 8:(h + 1) * 8, :], max_dma_last_dim=2048),
                                writes=[dst.d(h)])
                    for src, dst in ((posk_d, posr[0]), (posv_d, posr[1]), (w2k_d, W2k), (w2v_d, W2v),
                                     (maskc_d, maskc), (maske_d, maske), (cmpm_d, cmpm),
                                     (vcinit_d, VCext), (fc_d, fct)):
                        Sc.dma("pool", lambda e: e.dma_start(out=dst[:], in_=src, max_dma_last_dim=2048), writes=[dst.d()])
                    Sc.op("pool", lambda e: e.memset(KTs[:], 0.0), writes=[KTs.d("w")])
                    Sc.dma("pool", lambda e: e.dma_start(out=KTs[64:96, 0, :], in_=wst_d, max_dma_last_dim=2048), writes=[KTs.d("w")])
                    Sc.dma("pool", lambda e: e.dma_start(out=KTs[0:32, 1, :], in_=wst_d, max_dma_last_dim=2048), writes=[KTs.d("w")])
                    Sc.dma("sp", lambda e: e.dma_start(out=invf[:], in_=invf_d), writes=[invf.d()])
                    Sc.dma("sp", lambda e: e.dma_start(out=esink[:], in_=sinks_d.partition_broadcast(128)),
                           writes=[esink.d()])
                    Sc.op("act", lambda e: e.activation(out=esink[:], in_=esink[:], func=AF.Exp),
                          reads=[esink.d()], writes=[esink.d()])
                    dump("onepT", onepT[:], onepT.d(), [128, 8, NB])
                    dump("shiftT", shiftT[:], shiftT.d(), [128, 8, NB])
                    barrier()
                    cp("p0")

                pbias = pmisc[:, 48:52].rearrange("p (a b) -> p a b", b=2)

                def mm_bias(e):
                    ins = None
                    for kv in range(2):
                        for fc in range(2):
                            for l2 in range(16):
                                ins = e.matmul(pbias[:, kv, fc:fc + 1], lhsT=W1[kv][:, l2, fc * 128:(fc + 1) * 128],
                                               rhs=posr[kv][:, l2:l2 + 1], start=(l2 == 0), stop=(l2 == 15))
                    return ins
                Sc.op("pe", mm_bias, reads=[W1[0].d(0), W1[0].d(1), W1[1].d(0), W1[1].d(1), posr[0].d(), posr[1].d()],
                      writes=[pmisc.d("bias")])
                Sc.op("dve", lambda e: e.tensor_copy(out=biasT[:], in_=pbias), reads=[pmisc.d("bias")], writes=[biasT.d()])
                Sc.op("dve", lambda e: e.tensor_scalar(out=nbiasT[:], in0=pbias, scalar1=-1.0, scalar2=None, op0=ALU.mult),
                      reads=[pmisc.d("bias")], writes=[nbiasT.d()])
                dump("biasT", biasT[:], biasT.d(), [128, 2, 2])
                cp("bias")

                with ExitStack() as pw:
                    xt = [sbuf(pw, "xt%d" % i, [128, D], F32) for i in range(2)]
                    uT = sbuf(pw, "uT", [128, 8, 384], BF16)
                    st6 = sbuf(pw, "st6", [128, 2, 6], F32)
                    mv = sbuf(pw, "mv", [128, 2], F32)
                    rstd = sbuf(pw, "rstd", [128, 1], F32)
                    nmr = sbuf(pw, "nmr", [128, 1], F32)
                    ra = sbuf(pw, "ra", [128, 512], F32)
                    rb = sbuf(pw, "rb", [128, 512], F32)
                    otmp = ra
                    qr = [sbuf(pw, "qr%d" % i, [128, 512], BF16) for i in range(2)]
                    qr.append(sbuf(pw, "qr2", [128, 512], BF16))
                    xdup = sbuf(pw, "xdup", [128, 4, 128], BF16)
                    QaT = sbuf(pw, "QaT", [128, 2, 4, 512], BF16)
                    QbT = sbuf(pw, "QbT", [128, 2, 4, 512], BF16)
                    PT = [sbuf(pw, "PT%d" % i, [128, 512], BF16) for i in range(4)]
                    hidT = sbuf(pw, "hidT", [128, 2, 2, 64], BF16)
                    sg = sbuf(pw, "sg", [128, 4, 24], F32)
                    oacc = sbuf(pw, "oacc", [128, 1024], F32)
                    rs = sbuf(pw, "rs", [128, 4], F32)
                    coef = sbuf(pw, "coef", [128, 4], F32)
                    imp = sbuf(pw, "imp", [128, 32], F32)
                    m8 = sbuf(pw, "m8", [128, 8], F32)
                    nsb = [sbuf(pw, "ns%d" % i, [128, 128], BF16) for i in range(2)]
                    Qsel = [sbuf(pw, "Qsel%d" % i, [128, 4, 128], BF16) for i in range(2)]
                    posf = sbuf(pw, "posf", [128, NT], F32)
                    posi = sbuf(pw, "posi", [128, NB, NT], I32)

                    Sc.dma("sp", lambda e: e.dma_start(out=posi[:], in_=pos_d), writes=[posi.d()])
                    Sc.op("pool", lambda e: e.memset(kcT[:], 0.0), writes=[kcT.d()])
                    Sc.op("pool", lambda e: e.memset(QaT[:], 0.0), writes=[QaT.d(i) for i in range(4)])
                    Sc.op("pool", lambda e: e.memset(QbT[:], 0.0), writes=[QbT.d(i) for i in range(4)])
                    for nb_ in nsb:
                        Sc.op("pool", lambda e: e.memset(nb_[:], 0.0), writes=[nb_.d()])
                    Sc.op("pool", lambda e: e.memset(XcT[:], 0.0), writes=[XcT.d()])
                    Sc.op("pool", lambda e: e.memset(VC[:], 1.0), writes=[VC.d()])

                    pst_i = [0]
                    pt_i = [0]
                    pend = []
                    qs_i = [0]
                    acc_i = [0]
                    a0_done = set()

                    for b in range(NB):
                        Sc.op("dve", lambda e, b=b: e.tensor_copy(out=posf[:], in_=posi[:, b, :]),
                              reads=[posi.d()], writes=[posf.d()])
                        angv = ra[:].rearrange("p (t j) -> p t j", j=32)
                        ang2v = rb[:].rearrange("p (t j) -> p t j", j=32)
                        kkv = oacc[:, 0:512].rearrange("p (t j) -> p t j", j=32)
                        Sc.op("dve", lambda e: e.tensor_tensor(
                            out=angv, in0=posf[:].unsqueeze(2).broadcast_to([128, NT, 32]),
                            in1=invf[:].unsqueeze(1).broadcast_to([128, NT, 32]), op=ALU.mult),
                            reads=[posf.d(), invf.d()], writes=[ra.d()])

                        def reduce_angle(src, sdep, dst, ddep):
                            Sc.op("dve", lambda e: e.tensor_scalar(out=kkv, in0=src, scalar1=1.0 / TWO_PI,
                                                                  scalar2=MAGIC, op0=ALU.mult, op1=ALU.add),
                                  reads=[sdep], writes=[oacc.d()])
                            Sc.op("dve", lambda e: e.tensor_scalar(out=kkv, in0=kkv, scalar1=-MAGIC, scalar2=None,
                                                                  op0=ALU.add), reads=[oacc.d()], writes=[oacc.d()])
                            Sc.op("dve", lambda e: e.scalar_tensor_tensor(out=dst, in0=kkv, scalar=-C1, in1=src,
                                                                         op0=ALU.mult, op1=ALU.add),
                                  reads=[oacc.d(), sdep], writes=[ddep])
                            Sc.op("dve", lambda e: e.scalar_tensor_tensor(out=dst, in0=kkv, scalar=-C2, in1=dst,
                                                                         op0=ALU.mult, op1=ALU.add),
                                  reads=[oacc.d(), ddep], writes=[ddep])
                            Sc.op("dve", lambda e: e.tensor_scalar(out=dst, in0=dst, scalar1=-PI_LO, scalar2=PI_LO,
                                                                  op0=ALU.max, op1=ALU.min), reads=[ddep], writes=[ddep])
                        Sc.op("dve", lambda e: e.tensor_scalar(out=ang2v, in0=angv, scalar1=0.5 * np.pi, scalar2=None,
                                                              op0=ALU.add), reads=[ra.d()], writes=[rb.d()])
                        reduce_angle(ang2v, rb.d(), ang2v, rb.d())
                        Sc.op("act", lambda e: e.activation(out=cs2[:], in_=ang2v, func=AF.Sin),
                              reads=[rb.d()], writes=[cs2.d()])
                        reduce_angle(angv, ra.d(), angv, ra.d())
                        Sc.op("act", lambda e: e.activation(out=sn2[:], in_=angv, func=AF.Sin),
                              reads=[ra.d()], writes=[sn2.d()])
                        if b == 0:
                            dump("cs2", cs2[:], cs2.d(), [128, NT, 32])
                            dump("sn2", sn2[:], sn2.d(), [128, NT, 32])
                        cp("tab")
                        if b > 0:
                            Sc.op("pool", lambda e: e.memset(XcT[:, :, 0:16], 0.0), reads=[], writes=[XcT.d()])

                        for s in range(NST):
                            if s > 0:
                                Sc.op("dve", lambda e: e.tensor_copy(out=XcT[0:64, :, 0:16], in_=XcT[0:64, :, 512:528]),
                                      reads=[XcT.d()], writes=[XcT.d()])
                                Sc.op("dve", lambda e: e.tensor_copy(out=XcT[64:128, :, 0:15], in_=XcT[64:128, :, 512:527]),
                                      reads=[XcT.d()], writes=[XcT.d()])
                            tpv = ptp[:].rearrange("p (a t) -> p a t", t=128)

                            def proj(pb, c0, c1, tl):
                                us_ = (4 * s + tl) % 3

                                def f(e):
                                    for kc in range(8):
                                        e.matmul(pb[:, 0:c1 - c0], lhsT=uT[:, kc, us_ * 128:(us_ + 1) * 128],
                                                 rhs=winP[:, kc, c0:c1], start=(kc == 0), stop=(kc == 7))
                                blk = (0, 512, 1024, 1536)[min(c0 // 512, 3)]
                                Sc.op("pe", f, reads=[uT.d(us_)] + [winP.d((blk, k)) for k in range(8)], writes=[pb.d()])

                            def rope(pb, dst, nh, tg):
                                v4 = lambda ap: ap.rearrange("p (h t j) -> p h t j", t=2, j=32)
                                cosb = cs2[:, tg, :].unsqueeze(1).unsqueeze(1).broadcast_to([128, nh, 2, 32])
                                sinb = sn2[:, tg, :].unsqueeze(1).unsqueeze(1).broadcast_to([128, nh, 2, 32])
                                W_ = nh * 64
                                pbv = pb[:, 0:W_]
                                swp = _mkap(pbv, 32, [[64, nh], [-32, 2], [1, 32]])
                                Sc.op("dve", lambda e: e.tensor_tensor(out=v4(ra[:, 0:W_]), in0=v4(pbv), in1=cosb, op=ALU.mult),
                                      reads=[pb.d(), cs2.d()], writes=[ra.d()])
                                Sc.op("dve", lambda e: e.tensor_tensor(out=v4(rb[:, 0:W_]), in0=swp, in1=sinb, op=ALU.mult),
                                      reads=[pb.d(), sn2.d()], writes=[rb.d()])
                                Sc.op("pool", lambda e: e.tensor_tensor(out=v4(dst[:, 0:W_])[:, :, 0, :], in0=v4(ra[:, 0:W_])[:, :, 0, :],
                                                                       in1=v4(rb[:, 0:W_])[:, :, 0, :], op=ALU.subtract),
                                      reads=[ra.d(), rb.d()], writes=[dst.d()])
                                Sc.op("pool", lambda e: e.tensor_tensor(out=v4(dst[:, 0:W_])[:, :, 1, :], in0=v4(ra[:, 0:W_])[:, :, 1, :],
                                                                       in1=v4(rb[:, 0:W_])[:, :, 1, :], op=ALU.add),
                                      reads=[ra.d(), rb.d()], writes=[dst.d()])

                            def stA0(tl, s_=None):
                                s__ = s if s_ is None else s_
                                tg = 4 * s__ + tl
                                if (b, tg) in a0_done:
                                    return
                                a0_done.add((b, tg))
                                row0 = (b * NT + tg) * 128
                                xb = xt[tg % 2]
                                Sc.dma("sp", lambda e: e.dma_start(out=xb[:, :], in_=x_d[row0:row0 + 128, :]), writes=[xb.d()])
                                ln_stats(xb, xb.d(), st6, mv[:], rstd[:], nmr[:], (st6.d(), mv.d(), rstd.d(), nmr.d()))
                                Sc.op("dve", lambda e: e.tensor_scalar(out=xb[:, :], in0=xb[:, :], scalar1=rstd[:], scalar2=nmr[:],
                                                                      op0=ALU.mult, op1=ALU.add),
                                      reads=[xb.d(), rstd.d(), nmr.d()], writes=[xb.d()])

                            def stA(tl):
                                tg = 4 * s + tl
                                xb = xt[tg % 2]
                                us_ = tg % 3
                                for half in range(2):
                                    pb = pj[half]

                                    def tr(e):
                                        for i in range(4):
                                            kc = half * 4 + i
                                            e.transpose(out=pb[:, i * 128:(i + 1) * 128], in_=xb[:, kc * 128:(kc + 1) * 128],
                                                        identity=identf[:])
                                    Sc.op("pe", tr, reads=[xb.d(), identf.d()], writes=[pb.d()])
                                    for i in range(4):
                                        kc = half * 4 + i
                                        Sc.op("act", lambda e: e.activation(
                                            out=uT[:, kc, us_ * 128:(us_ + 1) * 128], in_=pb[:, i * 128:(i + 1) * 128], func=AF.Identity,
                                            scale=onepT[:, kc, b:b + 1], bias=shiftT[:, kc, b:b + 1]),
                                            reads=[pb.d(), onepT.d(), shiftT.d()], writes=[uT.d(us_)])

                            def stB(tl):
                                tg = 4 * s + tl
                                proj(pst[0], 0, 512, tl)
                                rope(pst[0], qr[0], 8, tg)
                                proj(pst[1], 512, 1024, tl)
                                rope(pst[1], qr[1], 8, tg)

                            def stB2(tl):
                                tg = 4 * s + tl

                                def tr_q(e):
                                    for qi, src in enumerate((qr[0], qr[1])):
                                        for hl in range(4):
                                            e.transpose(out=tpv[:, qi * 4 + hl, :], in_=src[:, hl * 128:(hl + 1) * 128], identity=ident[:])
                                Sc.op("pe", tr_q, reads=[qr[0].d(), qr[1].d(), ident.d()], writes=[ptp.d()])
                                for g_ in range(2):
                                    gs_ = slice(g_ * 64, (g_ + 1) * 64)
                                    Sc.op("dve", lambda e: e.tensor_copy(out=QaT[gs_, g_, :, tl * 128:(tl + 1) * 128], in_=tpv[gs_, 0:4, :]),
                                          reads=[ptp.d()], writes=[QaT.d(tl)])
                                    Sc.op("dve", lambda e: e.tensor_copy(out=QbT[gs_, g_, :, tl * 128:(tl + 1) * 128], in_=tpv[gs_, 4:8, :]),
                                          reads=[ptp.d()], writes=[QbT.d(tl)])

                            def stC(tl):
                                tg = 4 * s + tl
                                proj(po[0], 1024, 1536, tl)
                                rope(po[0], qr[2], 8, tg)
                                proj(po[1], 1536, 2048, tl)
                                Sc.op("act", lambda e: e.copy(
                                    out=xdup[:, 2:4, :].rearrange("p g (r d) -> p g r d", r=2),
                                    in_=po[1][:, 0:128].rearrange("p (g d) -> p g d", g=2).unsqueeze(2).broadcast_to([128, 2, 2, 64])),
                                    reads=[po[1].d()], writes=[xdup.d("v")])
                                Sc.op("pool", lambda e: e.tensor_copy(
                                    out=xdup[:, 0:2, :].rearrange("p g (r d) -> p g r d", r=2),
                                    in_=qr[2][:, 0:128].rearrange("p (g d) -> p g d", g=2).unsqueeze(2).broadcast_to([128, 2, 2, 64])),
                                    reads=[qr[2].d()], writes=[xdup.d("k")])
                                Sc.op("act", lambda e: e.copy(
                                    out=VC[:, tg, :, :, 0:64],
                                    in_=po[1][:, 128:512].rearrange("p (k g d) -> p k g d", k=3, g=2)),
                                    reads=[po[1].d()], writes=[VC.d(tg)])
                                pg = pmisc[:, 64:88]

                                us_ = tg % 3

                                def mm_g(e):
                                    for kc in range(8):
                                        e.matmul(pg, lhsT=uT[:, kc, us_ * 128:(us_ + 1) * 128], rhs=winP[:, kc, 2048:2072],
                                                 start=(kc == 0), stop=(kc == 7))
                                Sc.op("pe", mm_g, reads=[uT.d(us_)] + [winP.d((1536, k)) for k in range(8)], writes=[pmisc.d("g")])
                                Sc.op("act", lambda e: e.activation(out=sg[:, tl, :], in_=pg, func=AF.Exp, scale=-1.0),
                                      reads=[pmisc.d("g")], writes=[sg.d(tl)])
                                Sc.op("dve", lambda e: e.tensor_scalar(out=sg[:, tl, :], in0=sg[:, tl, :], scalar1=1.0, scalar2=None,
                                                                      op0=ALU.add), reads=[sg.d(tl)], writes=[sg.d(tl)])
                                Sc.op("dve", lambda e: e.reciprocal(out=sg[:, tl, :], in_=sg[:, tl, :]), reads=[sg.d(tl)], writes=[sg.d(tl)])

                            def stC2(tl):
                                tg = 4 * s + tl

                                def tr_k(e):
                                    for i4 in range(4):
                                        e.transpose(out=tpv[:, i4, :], in_=xdup[:, i4, :], identity=ident[:])
                                    for k in range(3):
                                        e.transpose(out=tpv[:, 4 + k, :], in_=qr[2][:, 128 * (k + 1):128 * (k + 2)], identity=ident[:])
                                Sc.op("pe", tr_k, reads=[qr[2].d(), xdup.d("k"), xdup.d("v"), ident.d()], writes=[ptp.d()])
                                Sc.op("act", lambda e: e.copy(out=KT[:, :, tg * 128:(tg + 1) * 128], in_=tpv[:, 5:7, :]),
                                      reads=[ptp.d()], writes=[KT.d(tg)])
                                Sc.op("act", lambda e: e.copy(out=KTs[0:64, 0, tg * 128:(tg + 1) * 128], in_=tpv[0:64, 4, :]),
                                      reads=[ptp.d()], writes=[KTs.d(tg)])
                                Sc.op("act", lambda e: e.copy(out=KTs[64:128, 1, tg * 128:(tg + 1) * 128], in_=tpv[64:128, 4, :]),
                                      reads=[ptp.d()], writes=[KTs.d(tg)])
                                Sc.op("dve", lambda e: e.tensor_copy(
                                    out=XcT[0:64, :, 16 + tl * 128:16 + (tl + 1) * 128], in_=tpv[0:64, 0:4, :]),
                                    reads=[ptp.d()], writes=[XcT.d()])
                                Sc.op("dve", lambda e: e.tensor_copy(
                                    out=XcT[64:128, :, 15 + tl * 128:15 + (tl + 1) * 128], in_=tpv[64:128, 0:4, :]),
                                    reads=[ptp.d()], writes=[XcT.d()])

                            stA0(0)
                            for step in range(4 + 1):
                                ok = lambda t_: 0 <= t_ < 4
                                if ok(step):
                                    stA(step)
                                if ok(step - 1):
                                    stC(step - 1)
                                    stB(step - 1)
                                if ok(step - 1):
                                    stC2(step - 1)
                                    stB2(step - 1)
                                if step + 1 < 4:
                                    stA0(step + 1)

                            cp("s2")
                            phs = [pmisc[:, 128 + 128 * i:256 + 128 * i].rearrange("p (a c) -> p a c", a=2) for i in range(2)]
                            for kv in range(2):
                                def mm_h(e):
                                    for fc in range(2):
                                        for l2 in range(16):
                                            rhs = _mkap(XcT[:, 2 * kv, :], 2 * l2, [[528, 2], [16, 32]])
                                            e.matmul(phs[kv][:, fc, :].rearrange("p (g c) -> p g c", g=2),
                                                     lhsT=W1[kv][:, l2, fc * 128:(fc + 1) * 128],
                                                     rhs=rhs, start=(l2 == 0), stop=(l2 == 15))
                                Sc.op("pe", mm_h, reads=[XcT.d(), W1[kv].d(0), W1[kv].d(1)], writes=[pmisc.d(("h", kv))])
                            for kv in range(2):
                                ph = phs[kv]
                                hx = rb[:, 0:256].rearrange("p (k a c) -> p k a c", k=2, a=2)[:, kv, :, :]
                                hd = hidT[:, kv, :, :]
                                for fc in range(2):
                                    Sc.op("act", lambda e: e.activation(
                                        out=hx[:, fc, :], in_=ph[:, fc, :], func=AF.Exp, bias=nbiasT[:, kv, fc:fc + 1], scale=-1.0),
                                        reads=[pmisc.d(("h", kv)), nbiasT.d()], writes=[rb.d()])
                                Sc.op("dve", lambda e: e.tensor_scalar(out=hx, in0=hx, scalar1=1.0, scalar2=None, op0=ALU.add),
                                      reads=[rb.d()], writes=[rb.d()])
                                Sc.op("dve", lambda e: e.reciprocal(out=hx, in_=hx), reads=[rb.d()], writes=[rb.d()])
                                for fc in range(2):
                                    Sc.op("dve", lambda e: e.scalar_tensor_tensor(
                                        out=hd[:, fc, :], in0=ph[:, fc, :], scalar=biasT[:, kv, fc:fc + 1], in1=hx[:, fc, :],
                                        op0=ALU.add, op1=ALU.mult), reads=[pmisc.d(("h", kv)), biasT.d(), rb.d()], writes=[hidT.d(kv)])
                            pk = pmisc[:, 384:448]

                            def mm_k(e):
                                for fc in range(2):
                                    e.matmul(pk, lhsT=W2k[:, fc, :], rhs=hidT[:, 0, fc, :], start=(fc == 0), stop=(fc == 1))
                            Sc.op("pe", mm_k, reads=[hidT.d(0), W2k.d()], writes=[pmisc.d("k")])
                            for g in range(2):
                                Sc.op("dve", lambda e: e.tensor_copy(
                                    out=kcT[g * 64:(g + 1) * 64, 32 * s:32 * s + 32], in_=pk[g * 64:(g + 1) * 64, g * 32:(g + 1) * 32]),
                                    reads=[pmisc.d("k")], writes=[kcT.d()])
                            for g in range(2):
                                pv = pmisc[0:32, 0:64] if g == 0 else pmisc[0:32, 448:512]

                                def mm_v(e):
                                    for fc in range(2):
                                        e.matmul(pv, lhsT=hidT[:, 1, fc, g * 32:(g + 1) * 32], rhs=W2v[:, fc, :],
                                                 start=(fc == 0), stop=(fc == 1))
                                Sc.op("pe", mm_v, reads=[hidT.d(1), W2v.d()], writes=[pmisc.d(("v", g))])
                                Sc.op("dve", lambda e: e.tensor_copy(out=VCext[32 * s:32 * s + 32, g, 0:64], in_=pv),
                                      reads=[pmisc.d(("v", g))], writes=[VCext.d()])
                            if b == 0 and s == 0:
                                dump("kcT0", kcT[:], kcT.d(), [128, 128], BF16)
                                dump("VCext0", VCext[:], VCext.d(), [128, 2, 97], BF16)

                            cp("s2b")
                            if s + 1 < NST:
                                stA0(0, s + 1)
                            for tl in range(4):
                                tg = 4 * s + tl
                                for g in range(2):
                                    gsl = slice(g * 64, (g + 1) * 64)
                                    qa_rhs = QaT[:, g, :, tl * 128:(tl + 1) * 128]
                                    qb_rhs = QbT[:, g, :, tl * 128:(tl + 1) * 128]

                                    def pair(lhsT_k, k_deps, q_rhs, q_dep, mask_kind, j, vrhs, v_deps, pacc, ncol, first, last,
                                             post=(), pre=()):
                                        for f in pre:
                                            f()
                                        stb = (pst[0], pst[1], pj[0], pmisc)[pst_i[0] % 4]
                                        pst_i[0] += 1
                                        ptb = PT[pt_i[0] % 4]
                                        pt_i[0] += 1

                                        def mm_s(e):
                                            e.matmul(stb[:].rearrange("p (h q) -> p h q", h=4), lhsT=lhsT_k, rhs=q_rhs,
                                                     start=True, stop=True)
                                        rd = list(k_deps) + [q_dep]
                                        Sc.op("pe", mm_s, reads=rd, writes=[stb.d()])
                                        Sc.op("act", lambda e: e.activation(out=ptb[:], in_=stb[:], func=AF.Exp, scale=0.125),
                                              reads=[stb.d()], writes=[ptb.d()])
                                        if mask_kind is not None:
                                            p3 = ptb[:].rearrange("p (h q) -> p h q", h=4)
                                            Sc.op("dve", lambda e: e.tensor_tensor(
                                                out=p3, in0=p3, in1=mask_kind.unsqueeze(1).broadcast_to([128, 4, 128]), op=ALU.mult),
                                                reads=[ptb.d(), maskc.d(), maske.d(), cmpm.d()], writes=[ptb.d()])

                                        def mm_pv(e):
                                            ins = None
                                            for hl in range(4):
                                                ins = e.matmul(pacc[:, hl * ncol:(hl + 1) * ncol], lhsT=ptb[:, hl * 128:(hl + 1) * 128],
                                                               rhs=vrhs, start=(first and hl == 0), stop=last,
                                                               skip_group_check=True)
                                            return ins

                                        def do_pv():
                                            Sc.op("pe", mm_pv, reads=[ptb.d()] + list(v_deps), writes=[pacc.d()])
                                            for f in post:
                                                f()
                                        while len(pend) >= 3:
                                            pend.pop(0)()
                                        pend.append(do_pv)

                                    def finish_branch(pacc, ncol, sgcol, dst_ap, accumulate, kind, g=g, tl=tl):
                                        pv3 = pacc[:, 0:4 * ncol].rearrange("p (h c) -> p h c", h=4)
                                        rsum = pv3[:, :, 64:65].rearrange("p h c -> p (h c)")
                                        if kind == "cmp":
                                            Sc.op("dve", lambda e: e.tensor_scalar(out=rs[:], in0=rsum, scalar1=1e-30, scalar2=None,
                                                                                  op0=ALU.max), reads=[pacc.d()], writes=[rs.d()])
                                        elif kind == "swa":
                                            Sc.op("dve", lambda e: e.tensor_tensor(out=rs[:], in0=rsum, in1=esink[:, g * 4:(g + 1) * 4],
                                                                                  op=ALU.add),
                                                  reads=[pacc.d(), esink.d()], writes=[rs.d()])
                                        else:
                                            Sc.op("dve", lambda e: e.tensor_copy(out=rs[:], in_=rsum), reads=[pacc.d()], writes=[rs.d()])
                                        Sc.op("dve", lambda e: e.reciprocal(out=rs[:], in_=rs[:]), reads=[rs.d()], writes=[rs.d()])
                                        if sgcol is not None:
                                            sgv = _mkap(sg[:, tl, :], g * 12 + sgcol, [[3, 4]])
                                            Sc.op("dve", lambda e: e.tensor_tensor(out=coef[:], in0=rs[:], in1=sgv, op=ALU.mult),
                                                  reads=[rs.d(), sg.d(tl)], writes=[coef.d()])
                                            cf = coef
                                        else:
                                            cf = rs
                                        cb = cf[:].unsqueeze(2).broadcast_to([128, 4, 64])
                                        if not accumulate:
                                            Sc.op("dve", lambda e: e.tensor_tensor(
                                                out=dst_ap.rearrange("p (h d) -> p h d", h=4), in0=pv3[:, :, 0:64], in1=cb, op=ALU.mult),
                                                reads=[pacc.d(), cf.d()], writes=[oacc.d()])
                                        else:
                                            Sc.op("dve", lambda e: e.tensor_tensor(
                                                out=otmp[:, 0:256].rearrange("p (h d) -> p h d", h=4), in0=pv3[:, :, 0:64], in1=cb, op=ALU.mult),
                                                reads=[pacc.d(), cf.d()], writes=[otmp.d()])
                                            Sc.op("pool", lambda e: e.tensor_tensor(out=dst_ap, in0=dst_ap, in1=otmp[:, 0:256], op=ALU.add),
                                                  reads=[otmp.d(), oacc.d()], writes=[oacc.d()])

                                    oa_dst = oacc[:, g * 256:(g + 1) * 256]
                                    ob_dst = oacc[:, 512 + g * 256:512 + (g + 1) * 256]

                                    accs = []
                                    for _ in range(4):
                                        accs.append((po[0], po[1], pj[1])[acc_i[0] % 3])
                                        acc_i[0] += 1
                                    acc_cmp, acc_win, acc_swa, acc_sel = accs
                                    nsg = nsb[g]
                                    nc0 = 64 if g == 0 else 0
                                    qsb = Qsel[qs_i[0] % 2]
                                    qs_i[0] += 1

                                    def select_blocks(tg=tg, nsg=nsg, nc0=nc0, acc_cmp=acc_cmp):
                                        pc3 = acc_cmp[:, 0:388].rearrange("p (h c) -> p h c", h=4)
                                        Sc.op("dve", lambda e: e.tensor_scalar(
                                            out=rs[:], in0=pc3[:, :, 64:65].rearrange("p h c -> p (h c)"), scalar1=1e-30, scalar2=None,
                                            op0=ALU.max), reads=[acc_cmp.d()], writes=[rs.d()])
                                        Sc.op("dve", lambda e: e.reciprocal(out=coef[:], in_=rs[:]), reads=[rs.d()], writes=[coef.d()])
                                        Sc.op("dve", lambda e: e.tensor_scalar(
                                            out=imp[:], in0=pc3[:, 0, 65:97], scalar1=coef[:, 0:1], scalar2=None, op0=ALU.mult),
                                            reads=[acc_cmp.d(), coef.d()], writes=[imp.d()])
                                        for hl in range(1, 4):
                                            Sc.op("dve", lambda e: e.scalar_tensor_tensor(
                                                out=imp[:], in0=pc3[:, hl, 65:97], scalar=coef[:, hl:hl + 1], in1=imp[:],
                                                op0=ALU.mult, op1=ALU.add), reads=[acc_cmp.d(), coef.d(), imp.d()], writes=[imp.d()])
                                        Sc.op("dve", lambda e: e.tensor_tensor(out=imp[:], in0=imp[:], in1=fct[:, tg, 0, :], op=ALU.max),
                                              reads=[imp.d(), fct.d()], writes=[imp.d()])
                                        Sc.op("dve", lambda e: e.tensor_tensor(out=imp[:], in0=imp[:], in1=fct[:, tg, 1, :], op=ALU.add),
                                              reads=[imp.d(), fct.d()], writes=[imp.d()])
                                        Sc.op("dve", lambda e: e.max(out=m8[:], in_=imp[:]), reads=[imp.d()], writes=[m8.d()])
                                        Sc.op("dve", lambda e: e.tensor_scalar(out=nsg[:, nc0:nc0 + 32], in0=imp[:], scalar1=m8[:, 7:8], scalar2=-1.0,
                                                                              op0=ALU.is_ge, op1=ALU.add),
                                              reads=[imp.d(), m8.d()], writes=[nsg.d()])

                                    def transpose_ns(nsg=nsg, nc0=nc0, qsb=qsb, qa_rhs=qa_rhs, tl=tl):
                                        ptn = ptp[:, 896:1024]
                                        Sc.op("dve", lambda e: e.tensor_copy(out=qsb[:], in_=qa_rhs), reads=[QaT.d(tl)], writes=[qsb.d()])
                                        Sc.op("pe", lambda e: e.transpose(out=ptn, in_=nsg[:], identity=ident[:]),
                                              reads=[nsg.d(), ident.d()], writes=[ptp.d("ns")])
                                        Sc.op("act", lambda e: e.copy(out=qsb[nc0:nc0 + 32, :, :],
                                                                      in_=ptn[nc0:nc0 + 32, :].unsqueeze(1).broadcast_to([32, 4, 128])),
                                              reads=[ptp.d("ns")], writes=[qsb.d()])

                                    pair(kcT[:, :], [kcT.d()], qa_rhs, QaT.d(tl), cmpm[:, tg, :], 0,
                                         VCext[:, g, :], [VCext.d()], acc_cmp, 97, True, True,
                                         post=[select_blocks,
                                               lambda oa_dst=oa_dst, fb=finish_branch, a_=acc_cmp: fb(a_, 97, 0, oa_dst, False, "cmp")])
                                    j0 = max(0, tg - 4)
                                    for j in range(j0, tg + 1):
                                        mk = maskc[:] if j == tg else (maske[:] if j == tg - 4 else None)
                                        pair(KT[:, 0, j * 128:(j + 1) * 128], [KT.d(j)], qa_rhs, QaT.d(tl), mk, j,
                                             VC[:, j, 1, g, :], [VC.d(j), VC.d()], acc_win, 65, j == j0, j == tg,
                                             post=([lambda oa_dst=oa_dst, fb=finish_branch, a_=acc_win: fb(a_, 65, 2, oa_dst, True, "win")] if j == tg else ()))
                                    j0 = max(0, tg - 1)
                                    for j in range(j0, tg + 1):
                                        mk = maskc[:] if j == tg else maske[:]
                                        pair(KT[:, 1, j * 128:(j + 1) * 128], [KT.d(j)], qb_rhs, QbT.d(tl), mk, j,
                                             VC[:, j, 2, g, :], [VC.d(j), VC.d()], acc_swa, 65, j == j0, j == tg,
                                             post=([lambda ob_dst=ob_dst, fb=finish_branch, a_=acc_swa: fb(a_, 65, None, ob_dst, False, "swa")] if j == tg else ()))
                                    for j in range(tg + 1):
                                        pst_ = []
                                        if j == tg:
                                            pst_.append(lambda oa_dst=oa_dst, fb=finish_branch, a_=acc_sel: fb(a_, 65, 1, oa_dst, True, "sel"))
                                            if g == 1:
                                                def fin(b=b, tg=tg):
                                                    Sc.op("act", lambda e: e.copy(out=OA[:, b * NT + tg, :], in_=oacc[:]),
                                                          reads=[oacc.d()], writes=[OA.d(b * NT + tg)])
                                                pst_.append(fin)
                                        pair(KTs[:, g, j * 128:(j + 1) * 128], [KTs.d(j), KTs.d("w")],
                                             qa_rhs if j == tg else qsb[:], QaT.d(tl) if j == tg else qsb.d(),
                                             maskc[:] if j == tg else None, j,
                                             VC[:, j, 0, g, :], [VC.d(j), VC.d()], acc_sel, 65, j == 0, j == tg,
                                             post=pst_, pre=([transpose_ns] if j == 0 else ()))
                            while pend:
                                pend.pop(0)()
                    barrier()
                    cp("P")

            with ExitStack() as qs:
                winQ = sbuf(qs, "winQ", [128, 8, QW], BF16)
                wupa = sbuf(qs, "wupa", [128, 4, D], BF16)
                wupb = sbuf(qs, "wupb", [128, 4, D], BF16)
                woutb = sbuf(qs, "woutb", [128, 8, D], BF16)
                lng = sbuf(qs, "lng", [128, D], F32)
                lnb = sbuf(qs, "lnb", [128, D], F32)
                gate_rep = sbuf(qs, "gate_rep", [128, NB, D], F32)

                winQ_v = winQ_d.rearrange("(k p) n -> p k n", p=128)
                for c0 in range(0, QW, 512):
                    for kc in range(8):
                        Sc.dma("pool", lambda e: e.dma_start(
                            out=winQ[:, kc, c0:c0 + 512], in_=winQ_v[:, kc, c0:c0 + 512], max_dma_last_dim=2048),
                            writes=[winQ.d((c0, kc))])
                for src, dst, nk in ((wupa_d, wupa, 4), (wupb_d, wupb, 4), (wout_d, woutb, 8)):
                    sv = src.rearrange("(k p) n -> p k n", p=128)
                    for kc in range(nk):
                        Sc.dma("pool", lambda e: e.dma_start(out=dst[:, kc, :], in_=sv[:, kc, :], max_dma_last_dim=2048),
                               writes=[dst.d(kc)])
                Sc.dma("sp", lambda e: e.dma_start(out=lng[:], in_=lng_d.partition_broadcast(128)), writes=[lng.d()])
                Sc.dma("sp", lambda e: e.dma_start(out=lnb[:], in_=lnb_d.partition_broadcast(128)), writes=[lnb.d()])

                with ExitStack() as qt:
                    gbc = sbuf(qt, "gbc", [128, 8, NB, 128], F32)
                    Sc.op("dve", lambda e: e.tensor_copy(out=gbc[:], in_=gateT[:].unsqueeze(3).broadcast_to([128, 8, NB, 128])),
                          reads=[gateT.d()], writes=[gbc.d()])
                    for b in range(NB):
                        for half in range(2):
                            pb = pj[half]

                            def mm_gr(e):
                                for i in range(4):
                                    e.matmul(pb[:, i * 128:(i + 1) * 128], lhsT=gbc[:, half * 4 + i, b, :], rhs=identf[:],
                                             start=True, stop=True)
                            Sc.op("pe", mm_gr, reads=[gbc.d(), identf.d()], writes=[pb.d()])
                            Sc.op("act", lambda e: e.copy(out=gate_rep[:, b, half * 512:(half + 1) * 512], in_=pb[:]),
                                  reads=[pb.d()], writes=[gate_rep.d()])
                    barrier()

                NXQ = 3
                xq = [sbuf(qs, "xq%d" % i, [128, D], F32) for i in range(NXQ)]
                xn = sbuf(qs, "xn", [128, D], F32)
                uTq = [sbuf(qs, "uTq%d" % i, [128, 8, 128], BF16) for i in range(2)]
                st6 = sbuf(qs, "st6q", [128, 2, 6], F32)
                mv = sbuf(qs, "mvq", [128, 2], F32)
                rstd = sbuf(qs, "rstdq", [128, 1], F32)
                nmr = sbuf(qs, "nmrq", [128, 1], F32)
                st6b = sbuf(qs, "st6qb", [128, 2, 6], F32)
                mvb = sbuf(qs, "mvqb", [128, 2], F32)
                rstdb = sbuf(qs, "rstdqb", [128, 1], F32)
                nmrb = sbuf(qs, "nmrqb", [128, 1], F32)
                szz = sbuf(qs, "szz", [128, 1024], BF16)
                sgm = sbuf(qs, "sgm", [128, 2048], BF16)
                yab = sbuf(qs, "yab", [128, 1024], BF16)
                yT = [sbuf(qs, "yT%d" % i, [128, 8, 128], BF16) for i in range(2)]
                m1 = sbuf(qs, "m1", [128, D], F32)
                m2 = sbuf(qs, "m2", [128, 512], F32)
                mg = sbuf(qs, "mg", [128, D], BF16)
                mT = [sbuf(qs, "mT%d" % i, [128, 8, 128], BF16) for i in range(2)]
                tt = sbuf(qs, "tt", [128, 512], F32)

                tpv = ptp[:].rearrange("p (a t) -> p a t", t=128)
                obanks = [pst[0], pst[1], po[0], po[1]]
                ob_i = [0]

                def nextbank():
                    bk = obanks[ob_i[0] % 4]
                    ob_i[0] += 1
                    return bk

                NTQ = NB * NT

                def Q1(ti):
                    b = ti // NT
                    row0 = ti * 128
                    xb = xq[ti % NXQ]
                    ub = uTq[ti % 2]
                    Sc.dma("sp", lambda e: e.dma_start(out=xb[:, :], in_=x_d[row0:row0 + 128, :]), writes=[xb.d()])
                    ln_stats(xb, xb.d(), st6, mv[:], rstd[:], nmr[:], (st6.d(), mv.d(), rstd.d(), nmr.d()))
                    Sc.op("dve", lambda e: e.tensor_scalar(out=xn[:, :], in0=xb[:, :], scalar1=rstd[:],
                                                          scalar2=nmr[:], op0=ALU.mult, op1=ALU.add),
                          reads=[xb.d(), rstd.d(), nmr.d()], writes=[xn.d()])
                    for half in range(2):
                        pb = pj[half]

                        def tr(e):
                            for i in range(4):
                                kc = half * 4 + i
                                e.transpose(out=pb[:, i * 128:(i + 1) * 128], in_=xn[:, kc * 128:(kc + 1) * 128],
                                            identity=identf[:])
                        Sc.op("pe", tr, reads=[xn.d(), identf.d()], writes=[pb.d()])
                        for i in range(4):
                            kc = half * 4 + i
                            Sc.op("act", lambda e: e.activation(
                                out=ub[:, kc, :], in_=pb[:, i * 128:(i + 1) * 128], func=AF.Identity,
                                scale=onepT[:, kc, b:b + 1], bias=shiftT[:, kc, b:b + 1]),
                                reads=[pb.d(), onepT.d(), shiftT.d()], writes=[ub.d()])

                def Q2(ti):
                    ub = uTq[ti % 2]
                    yt_ = yT[ti % 2]
                    for i in range(2):
                        pb = nextbank()
                        c0 = i * 512

                        def f(e):
                            for kc in range(8):
                                e.matmul(pb[:], lhsT=ub[:, kc, :], rhs=winQ[:, kc, c0:c0 + 512],
                                         start=(kc == 0), stop=(kc == 7))
                        Sc.op("pe", f, reads=[ub.d()] + [winQ.d((c0, k)) for k in range(8)], writes=[pb.d()])
                        Sc.op("act", lambda e: e.activation(out=szz[:, i * 512:(i + 1) * 512], in_=pb[:], func=AF.Silu),
                              reads=[pb.d()], writes=[szz.d(i)])
                    Sc.op("dve", lambda e: e.tensor_tensor(out=yab[:], in0=OA[:, ti, :], in1=szz[:], op=ALU.mult),
                          reads=[OA.d(ti), szz.d(0), szz.d(1)], writes=[yab.d()])

                    def tr_y(e):
                        for i in range(8):
                            e.transpose(out=tpv[:, i, :], in_=yab[:, i * 128:(i + 1) * 128], identity=ident[:])
                    Sc.op("pe", tr_y, reads=[yab.d(), ident.d()], writes=[ptp.d()])
                    Sc.op("act", lambda e: e.copy(out=yt_[:], in_=tpv), reads=[ptp.d()], writes=[yt_.d()])

                def Q3(ti):
                    ub = uTq[ti % 2]
                    yt_ = yT[ti % 2]
                    mt_ = mT[ti % 2]
                    for i in range(4):
                        pb = nextbank()
                        c0 = 1024 + i * 512

                        def f(e):
                            for kc in range(8):
                                e.matmul(pb[:], lhsT=ub[:, kc, :], rhs=winQ[:, kc, c0:c0 + 512],
                                         start=(kc == 0), stop=(kc == 7))
                        Sc.op("pe", f, reads=[ub.d()] + [winQ.d((c0, k)) for k in range(8)], writes=[pb.d()])
                        Sc.op("act", lambda e: e.activation(out=sgm[:, i * 512:(i + 1) * 512], in_=pb[:], func=AF.Sigmoid),
                              reads=[pb.d()], writes=[sgm.d(i)])
                    for n in range(2):
                        for ab, wu in enumerate((wupa, wupb)):
                            pb = nextbank()

                            def f(e):
                                for kc in range(4):
                                    e.matmul(pb[:], lhsT=yt_[:, ab * 4 + kc, :], rhs=wu[:, kc, n * 512:(n + 1) * 512],
                                             start=(kc == 0), stop=(kc == 3))
                            Sc.op("pe", f, reads=[yt_.d()] + [wu.d(k) for k in range(4)], writes=[pb.d()])
                            dst = m1[:, n * 512:(n + 1) * 512] if ab == 0 else m2[:]
                            ddep = m1.d(n) if ab == 0 else m2.d()
                            Sc.op("dve", lambda e: e.tensor_tensor(
                                out=dst, in0=pb[:],
                                in1=sgm[:, ab * 1024 + n * 512:ab * 1024 + (n + 1) * 512], op=ALU.mult),
                                reads=[pb.d(), sgm.d(ab * 2 + n)], writes=[ddep])
                        Sc.op("pool", lambda e: e.tensor_tensor(out=mg[:, n * 512:(n + 1) * 512], in0=m1[:, n * 512:(n + 1) * 512],
                                                               in1=m2[:], op=ALU.add),
                              reads=[m1.d(n), m2.d()], writes=[mg.d(n)])

                    def tr_m(e):
                        for i in range(8):
                            e.transpose(out=tpv[:, i, :], in_=mg[:, i * 128:(i + 1) * 128], identity=ident[:])
                    Sc.op("pe", tr_m, reads=[mg.d(0), mg.d(1), ident.d()], writes=[ptp.d()])
                    Sc.op("act", lambda e: e.copy(out=mt_[:], in_=tpv), reads=[ptp.d()], writes=[mt_.d()])

                def Q4(ti):
                    b = ti // NT
                    row0 = ti * 128
                    xb = xq[ti % NXQ]
                    mt_ = mT[ti % 2]
                    for n in range(2):
                        pb = nextbank()

                        def f(e):
                            for kc in range(8):
                                e.matmul(pb[:], lhsT=mt_[:, kc, :], rhs=woutb[:, kc, n * 512:(n + 1) * 512],
                                         start=(kc == 0), stop=(kc == 7))
                        Sc.op("pe", f, reads=[mt_.d()] + [woutb.d(k) for k in range(8)], writes=[pb.d()])
                        Sc.op("dve", lambda e: e.tensor_tensor(
                            out=tt[:], in0=pb[:], in1=gate_rep[:, b, n * 512:(n + 1) * 512], op=ALU.mult),
                            reads=[pb.d(), gate_rep.d()], writes=[tt.d()])
                        Sc.op("dve", lambda e: e.scalar_tensor_tensor(out=xb[:, n * 512:(n + 1) * 512], in0=xb[:, n * 512:(n + 1) * 512],
                                                                     scalar=ALPHA, in1=tt[:], op0=ALU.mult, op1=ALU.add),
                              reads=[xb.d(), tt.d()], writes=[xb.d()])
                    ln_stats(xb, xb.d(), st6b, mvb[:], rstdb[:], nmrb[:], (st6b.d(), mvb.d(), rstdb.d(), nmrb.d()))
                    Sc.op("act", lambda e: e.activation(out=xb[:, :], in_=xb[:, :], func=AF.Identity, scale=rstdb[:],
                                                       bias=nmrb[:]), reads=[xb.d(), rstdb.d(), nmrb.d()], writes=[xb.d()])
                    Sc.op("pool", lambda e: e.tensor_tensor(out=xb[:, :], in0=xb[:, :], in1=lng[:], op=ALU.mult),
                          reads=[xb.d(), lng.d()], writes=[xb.d()])
                    Sc.op("pool", lambda e: e.tensor_tensor(out=xb[:, :], in0=xb[:, :], in1=lnb[:], op=ALU.add),
                          reads=[xb.d(), lnb.d()], writes=[xb.d()])
                    Sc.dma("sp", lambda e: e.dma_start(out=out_d[row0:row0 + 128, :], in_=xb[:, :]), reads=[xb.d()])

                stages = [Q1, Q2, Q3, Q4]
                for step in range(NTQ + 3):
                    for k in (3, 2, 1, 0):
                        ti = step - k
                        if 0 <= ti < NTQ:
                            stages[k](ti)

        try:
            body()
        except _Stop:
            pass
        Sc.finish()
        with nc.Block() as block:
            Sc.emit(block)
    return nc, dbg_outs


_IN_SPLITS = [512, 256, 256, 256, 24, 512, 512, 256, 512, 2048]


def _constants():
    k = np.arange(128)[:, None]
    q = np.arange(128)[None, :]
    neg = np.float32(-NEGM)
    c = {}
    c["c_ident"] = np.eye(128, dtype=np.float32)
    c["c_maskc"] = np.where(k <= q, 1.0, 0.0).astype(np.float32)
    c["c_maske"] = np.where(k > q, 1.0, 0.0).astype(np.float32)
    cm = np.zeros((128, NT, 128), np.float32)
    for t in range(NT):
        vis = (16 * k + 15 <= 128 * t + q) & (k >= 1)
        cm[:, t, :] = np.where(vis, 1.0, 0.0)
    c["c_cmpm"] = cm
    blk = np.arange(32)[:, None]
    m = np.arange(S)[None, :]
    c["c_wst"] = np.where(m // 64 == blk, np.float32(NEGM), 0.0).astype(np.float32)
    fc = np.zeros((128, NT, 2, 32), np.float32)
    j = np.arange(32)[None, :]
    for t in range(NT):
        cur = (128 * t + np.arange(128)[:, None]) // 64
        forced = (j == 0) | (j == cur) | (j == cur - 1)
        fc[:, t, 0, :] = np.where(forced, 1e9, 0.0)
        fc[:, t, 1, :] = np.where(j <= cur, 0.0, -1e30)
    c["c_fc"] = fc
    vci = np.zeros((128, 2, 97), np.float32)
    vci[:, :, 64] = 1.0
    cp = np.arange(128)[:, None]
    c0 = (cp - 1) * 16
    j0 = np.arange(32)[None, :] * 64
    ov = (c0 < j0 + 64) & (c0 + 32 > j0) & (cp >= 1)
    vci[:, 0, 65:97] = ov
    vci[:, 1, 65:97] = ov
    c["c_vcinit"] = vci
    half = 32
    invf = (np.float32(10000.0) ** (-np.arange(half, dtype=np.float32) / np.float32(half))).astype(np.float32)
    c["c_invf"] = np.ascontiguousarray(np.broadcast_to(invf[None, :], (128, 32))).astype(np.float32)
    return c


def _prepare_inputs(x, c, positions, w_ada, b_ada, w_in, cmp_pos_k, cmp_w1_k, cmp_w2_k,
                    cmp_pos_v, cmp_w1_v, cmp_w2_v, sinks, w_up_a, w_up_b, w_out, ln_g, ln_b):
    f32 = lambda a: np.ascontiguousarray(np.asarray(a, dtype=np.float32))
    x = f32(x); c = f32(c); positions = np.asarray(positions).astype(np.int32)
    w_ada = f32(w_ada)[0]; b_ada = f32(b_ada)[0]; w_in = f32(w_in)[0]
    offs = np.cumsum([0] + _IN_SPLITS)
    col = lambda i: w_in[:, offs[i]:offs[i + 1]]
    q_a, kv_cmp, kv_sel, kv_win, g_nsa, z_a, q_b, kv_b, z_b, g_merge = [col(i) for i in range(10)]
    qperm = lambda t: t.reshape(D, 2, 4, 64).transpose(0, 2, 1, 3).reshape(D, 512)
    q_a = qperm(q_a); q_b = qperm(q_b)
    kpart = lambda t: t[:, :128]
    vpart = lambda t: t[:, 128:]
    w_inP = np.concatenate([q_a, q_b, kpart(kv_cmp), kpart(kv_sel), kpart(kv_win), kpart(kv_b),
                            vpart(kv_cmp), vpart(kv_sel), vpart(kv_win), vpart(kv_b), g_nsa], axis=1)
    w_inQ = np.concatenate([z_a, z_b, g_merge], axis=1)
    assert w_inP.shape[1] == PW and w_inQ.shape[1] == QW

    def w1_layout(w1):
        return np.ascontiguousarray(f32(w1)[0].reshape(16, 2, 64, 256).transpose(1, 2, 0, 3).reshape(128, 16, 256))

    def pos_layout(p):
        return np.ascontiguousarray(f32(p)[0].reshape(16, 2, 64).transpose(1, 2, 0).reshape(128, 16))
    w2k = f32(cmp_w2_k)[0].reshape(2, 128, 64).transpose(1, 0, 2)
    w2k = np.ascontiguousarray(np.concatenate([w2k, w2k], axis=2))
    w2v = np.ascontiguousarray(f32(cmp_w2_v)[0].reshape(2, 128, 64).transpose(1, 0, 2))
    shared = {
        "w_ada": w_ada, "b_adaT": np.ascontiguousarray(b_ada.reshape(24, 128).T),
        "b_gate": np.ascontiguousarray(b_ada[None, 2048:]),
        "w_inP": np.ascontiguousarray(w_inP), "w_inQ": np.ascontiguousarray(w_inQ),
        "w1k": w1_layout(cmp_w1_k), "w1v": w1_layout(cmp_w1_v),
        "posk": pos_layout(cmp_pos_k), "posv": pos_layout(cmp_pos_v),
        "w2k": w2k, "w2v": w2v, "sinks": f32(sinks),
        "w_up_a": f32(w_up_a)[0], "w_up_b": f32(w_up_b)[0], "w_out": f32(w_out)[0],
        "ln_g": f32(ln_g), "ln_b": f32(ln_b),
    }
    shared.update(_constants())
    in_maps = []
    for i in range(8):
        m = dict(shared)
        m["x"] = np.ascontiguousarray(x[NB * i:NB * (i + 1)].reshape(NB * S, D))
        cc = c[NB * i:NB * (i + 1)]
        m["cT"] = np.ascontiguousarray(cc.T.reshape(8, 128, NB).transpose(1, 0, 2))
        pp = positions[NB * i:NB * (i + 1)]
        m["posT"] = np.ascontiguousarray(pp.reshape(NB, NT, 128).transpose(2, 0, 1))
        in_maps.append(m)
    return in_maps


_PROGRAM = {}


def kernel(**inputs):
    in_maps = _prepare_inputs(**inputs)
    if "nc" not in _PROGRAM:
        _PROGRAM["nc"] = build_program()[0]
    res = run_bass_kernel_spmd(_PROGRAM["nc"], in_maps, core_ids=list(range(8)))
    outs = [np.asarray(r["out"]).reshape(NB, S, D) for r in res.results]
    return np.concatenate(outs, axis=0).astype(np.float32)
```

```python
import numpy as np
from contextlib import ExitStack
import concourse.bass as bass
import concourse.mybir as mybir
from concourse.bass_utils import run_bass_kernel_spmd

F32 = mybir.dt.float32
BF16 = mybir.dt.bfloat16
I32 = mybir.dt.int32
AF = mybir.ActivationFunctionType
ALU = mybir.AluOpType
AX = mybir.AxisListType


class Dep:
    __slots__ = ("w", "readers", "name", "bank", "last")

    def __init__(self, name="", bank=None):
        self.w = None
        self.readers = {}
        self.name = name
        self.bank = bank
        self.last = None


class _Rec:
    def __init__(self):
        self.calls = []

    def __getattr__(self, name):
        def f(*a, **k):
            self.calls.append((name, a, k))
            return None
        return f


class Sched:
    ENG = ("pe", "act", "dve", "pool", "sp")

    def __init__(self, nc, stack, n_dma_sems=8):
        self.nc = nc
        self.stack = stack
        self.eng_obj = {"pe": nc.tensor, "act": nc.scalar, "dve": nc.vector,
                        "pool": nc.gpsimd, "sp": nc.sync}
        self.sems = {}
        for e in self.ENG:
            self.sems[e] = stack.enter_context(nc.semaphore("s_" + e))
        self.count = {e: 0 for e in self.ENG}
        self.prog = {e: [] for e in self.ENG}
        self.seen = {e: {} for e in self.ENG}
        self.dma_sems = {}
        self.dma_uses = {}
        self.dma_rr = {}
        for q in ("sp", "pool", "act"):
            lst = []
            for i in range(n_dma_sems):
                k = "d_%s%d" % (q, i)
                self.sems[k] = stack.enter_context(nc.semaphore(k))
                self.dma_uses[k] = 0
                lst.append(k)
            self.dma_sems[q] = lst
            self.dma_rr[q] = 0

    def _collect(self, eng, reads, writes, extra=()):
        waits = {}

        def add(tok):
            if tok is None:
                return
            k, v = tok
            if waits.get(k, 0) < v:
                waits[k] = v
        for d in reads:
            add(d.w)
        for d in writes:
            add(d.w)
            for k, v in d.readers.items():
                add((k, v))
        for t in extra:
            add(t)
        for d in list(reads) + list(writes):
            bk = d.bank
            if bk is not None and bk.last is not None and bk.last[0] != eng:
                add(bk.last)
        out = []
        seen = self.seen[eng]
        for k, v in waits.items():
            if k == eng:
                if eng == "pe" or self.count[eng] - v >= 3:
                    continue
            if seen.get(k, 0) >= v:
                continue
            seen[k] = v
            out.append((k, v))
        return out

    def _commit(self, tok, reads, writes):
        k, v = tok
        for d in reads:
            if d.readers.get(k, 0) < v:
                d.readers[k] = v
        for d in writes:
            d.w = tok
            d.readers = {}
        for d in list(reads) + list(writes):
            if d.bank is not None:
                d.bank.last = tok

    muted = False

    def op(self, eng, fn, reads=(), writes=()):
        if self.muted:
            return None
        waits = self._collect(eng, reads, writes)
        self.count[eng] += 1
        tok = (eng, self.count[eng])
        rec = _Rec()
        fn(rec)
        assert rec.calls
        self.prog[eng].append((waits, rec.calls, (eng, 1)))
        self._commit(tok, reads, writes)
        return tok

    def dma(self, q, fn, reads=(), writes=()):
        if self.muted:
            return None
        lst = self.dma_sems[q]
        k = lst[self.dma_rr[q] % len(lst)]
        self.dma_rr[q] += 1
        prev = (k, 16 * self.dma_uses[k]) if self.dma_uses[k] else None
        waits = self._collect(q, reads, writes, extra=(prev,) if prev else ())
        self.dma_uses[k] += 1
        tok = (k, 16 * self.dma_uses[k])
        rec = _Rec()
        fn(rec)
        assert len(rec.calls) == 1
        self.prog[q].append((waits, rec.calls, (k, 16)))
        self._commit(tok, reads, writes)
        return tok

    def finish(self):
        finals = []
        for e in self.ENG:
            if self.count[e]:
                finals.append((e, self.count[e]))
        for k, u in self.dma_uses.items():
            if u:
                finals.append((k, 16 * u))
        self.prog["sp"].append((finals, None, None))

    def emit(self, block):
        sems = self.sems

        def mk(e):
            prog = self.prog[e]

            def body(eng):
                for waits, fn, inc in prog:
                    for k, v in waits:
                        eng.wait_ge(sems[k], v)
                    if fn is not None:
                        ins = None
                        for name, a, k in fn:
                            ins = getattr(eng, name)(*a, **k)
                        ins.then_inc(sems[inc[0]], inc[1])
            return body
        block.tensor(mk("pe"))
        block.scalar(mk("act"))
        block.vector(mk("dve"))
        block.gpsimd(mk("pool"))
        block.sync(mk("sp"))


D = 1024
S = 2048
NB = 2
NT = S // 128
NST = NT // 4
LN_EPS = 1e-5
ALPHA = (2 * 1) ** 0.25
NEGM = 32768.0
PW = 2072
QW = 3072
MAGIC = 12582912.0
TWO_PI = 6.283185307179586
C1 = 6.28125
C2 = TWO_PI - C1
PI_LO = 3.1415925


def _mkap(base, off, dims):
    return bass.AP(base.tensor, base.offset + off, [list(base.ap[0])] + [list(d) for d in dims])


class Buf:
    def __init__(self, t, name, psum=False):
        self.t = t
        self.name = name
        self.deps = {}
        self.bank = Dep(name + "/bank") if psum else None

    def __getitem__(self, k):
        return self.t[k]

    def d(self, key=None):
        if key not in self.deps:
            self.deps[key] = Dep("%s/%s" % (self.name, key), bank=self.bank)
        return self.deps[key]


class _Stop(Exception):
    pass


def build_program(debug=None, stop=None):
    nc = bass.Bass("TRN2", target_bir_lowering=False)
    dbg_outs = {}

    muted = [False]

    def cp(name):
        if stop is not None and stop == name:
            muted[0] = True
            Sc_holder[0].muted = True
    Sc_holder = [None]

    def din(name, shape, dt=F32):
        return nc.dram_tensor(name, list(shape), dt, kind="ExternalInput").ap()

    x_d = din("x", [NB * S, D])
    out_d = nc.dram_tensor("out", [NB * S, D], F32, kind="ExternalOutput").ap()
    cT_d = din("cT", [128, 8, NB])
    pos_d = din("posT", [128, NB, NT], I32)
    wada_d = din("w_ada", [D, 3 * D])
    badaT_d = din("b_adaT", [128, 24])
    bgate_d = din("b_gate", [1, D])
    winP_d = din("w_inP", [D, PW])
    winQ_d = din("w_inQ", [D, QW])
    w1k_d = din("w1k", [128, 16, 256])
    w1v_d = din("w1v", [128, 16, 256])
    posk_d = din("posk", [128, 16])
    posv_d = din("posv", [128, 16])
    w2k_d = din("w2k", [128, 2, 128])
    w2v_d = din("w2v", [128, 2, 64])
    sinks_d = din("sinks", [1, 8])
    wupa_d = din("w_up_a", [512, D])
    wupb_d = din("w_up_b", [512, D])
    wout_d = din("w_out", [D, D])
    lng_d = din("ln_g", [1, D])
    lnb_d = din("ln_b", [1, D])
    ident_d = din("c_ident", [128, 128])
    maskc_d = din("c_maskc", [128, 128])
    maske_d = din("c_maske", [128, 128])
    cmpm_d = din("c_cmpm", [128, NT, 128])
    wst_d = din("c_wst", [32, S])
    fc_d = din("c_fc", [128, NT, 2, 32])
    vcinit_d = din("c_vcinit", [128, 2, 97])
    invf_d = din("c_invf", [128, 32])

    with ExitStack() as gs:
        Sc = Sched(nc, gs)
        Sc_holder[0] = Sc

        def sbuf(st, name, shape, dt):
            return Buf(st.enter_context(nc.sbuf_tensor("sb_" + name, list(shape), dt)), name)

        def psum(st, name, shape, dt):
            return Buf(st.enter_context(nc.psum_tensor("ps_" + name, list(shape), dt)), name, psum=True)

        def dump(name, buf_ap, dep, shape, dt=F32):
            if debug is None or name not in debug:
                return
            o = nc.dram_tensor("dbg_" + name, list(shape), dt, kind="ExternalOutput").ap()
            dbg_outs[name] = o
            Sc.dma("sp", lambda e: e.dma_start(out=o, in_=buf_ap), reads=[dep])

        def barrier():
            if Sc.muted:
                return
            toks = [(e, Sc.count[e]) for e in Sc.ENG if Sc.count[e]]
            toks += [(k, 16 * u) for k, u in Sc.dma_uses.items() if u]
            for e in Sc.ENG:
                w = []
                for k, v in toks:
                    if k == e:
                        continue
                    if Sc.seen[e].get(k, 0) < v:
                        Sc.seen[e][k] = v
                        w.append((k, v))
                if w:
                    Sc.prog[e].append((w, None, None))

        def body():
            pj = [psum(gs, "pj%d" % i, [128, 512], F32) for i in range(2)]
            pst = [psum(gs, "pst%d" % i, [128, 512], F32) for i in range(2)]
            po = [psum(gs, "po%d" % i, [128, 512], F32) for i in range(2)]
            ptp = psum(gs, "ptp", [128, 1024], BF16)
            pmisc = psum(gs, "pmisc", [128, 512], F32)

            OA = sbuf(gs, "OA", [128, NB * NT, 1024], BF16)
            gateT = sbuf(gs, "gateT", [128, 8, NB], F32)
            onepT = sbuf(gs, "onepT", [128, 8, NB], F32)
            shiftT = sbuf(gs, "shiftT", [128, 8, NB], F32)
            ident = sbuf(gs, "ident", [128, 128], BF16)
            identf = sbuf(gs, "identf", [128, 128], F32)
            epsb = sbuf(gs, "epsb", [128, 1], F32)

            Sc.dma("pool", lambda e: e.dma_start(out=ident[:], in_=ident_d), writes=[ident.d()])
            Sc.dma("sp", lambda e: e.dma_start(out=identf[:], in_=ident_d), writes=[identf.d()])
            Sc.op("dve", lambda e: e.memset(epsb[:], LN_EPS), writes=[epsb.d()])

            def ln_stats(xt, xdep, st6, mv, rstd, nmr, tagdeps):
                sd, md, rd, nd = tagdeps
                Sc.op("dve", lambda e: e.bn_stats(out=st6[:, 0, :], in_=xt[:, 0:512]), reads=[xdep], writes=[sd])
                Sc.op("dve", lambda e: e.bn_stats(out=st6[:, 1, :], in_=xt[:, 512:1024]), reads=[xdep], writes=[sd])
                Sc.op("dve", lambda e: e.bn_aggr(out=mv, in_=st6[:].rearrange("p a b -> p (a b)")), reads=[sd], writes=[md])
                Sc.op("act", lambda e: e.activation(out=rstd, in_=mv[:, 1:2], func=AF.Ln, bias=epsb[:], scale=1.0),
                      reads=[md, epsb.d()], writes=[rd])
                Sc.op("act", lambda e: e.activation(out=rstd, in_=rstd, func=AF.Exp, scale=-0.5), reads=[rd], writes=[rd])
                Sc.op("dve", lambda e: e.tensor_scalar(out=nmr, in0=mv[:, 0:1], scalar1=rstd, scalar2=-1.0,
                                                      op0=ALU.mult, op1=ALU.mult), reads=[md, rd], writes=[nd])

            def ln_mod_transpose(xb, xdep, b, uT_ap_fn, uT_dep, small, smalldeps):
                st6, mv, rstd, nmr = small
                ln_stats(xb, xdep, st6, mv, rstd, nmr, smalldeps)
                sd, md, rd, nd = smalldeps
                Sc.op("dve", lambda e: e.tensor_scalar(out=xb[:, :], in0=xb[:, :], scalar1=rstd, scalar2=nmr,
                                                      op0=ALU.mult, op1=ALU.add), reads=[xdep, rd, nd], writes=[xdep])
                for half in range(2):
                    pb = pj[half]

                    def tr(e, half=half, pb=pb):
                        ins = None
                        for i in range(4):
                            kc = half * 4 + i
                            ins = e.transpose(out=pb[:, i * 128:(i + 1) * 128], in_=xb[:, kc * 128:(kc + 1) * 128],
                                              identity=identf[:])
                        return ins
                    Sc.op("pe", tr, reads=[xdep, identf.d()], writes=[pb.d()])
                    for i in range(4):
                        kc = half * 4 + i
                        Sc.op("act", lambda e, kc=kc, i=i, pb=pb: e.activation(
                            out=uT_ap_fn(kc), in_=pb[:, i * 128:(i + 1) * 128], func=AF.Identity,
                            scale=onepT[:, kc, b:b + 1], bias=shiftT[:, kc, b:b + 1]),
                            reads=[pb.d(), onepT.d(), shiftT.d()], writes=[uT_dep])

            with ExitStack() as ps_:
                winP = sbuf(ps_, "winP", [128, 8, PW], BF16)
                W1 = [sbuf(ps_, "W1k", [128, 16, 256], BF16), sbuf(ps_, "W1v", [128, 16, 256], BF16)]
                posr = [sbuf(ps_, "posk", [128, 16], BF16), sbuf(ps_, "posv", [128, 16], BF16)]
                W2k = sbuf(ps_, "W2k", [128, 2, 128], BF16)
                W2v = sbuf(ps_, "W2v", [128, 2, 64], BF16)
                biasT = sbuf(ps_, "biasT", [128, 2, 2], F32)
                nbiasT = sbuf(ps_, "nbiasT", [128, 2, 2], F32)
                KT = sbuf(ps_, "KT", [128, 2, S], BF16)
                KTs = sbuf(ps_, "KTs", [128, 2, S], BF16)
                VC = sbuf(ps_, "VC", [128, NT, 3, 2, 65], BF16)
                XcT = sbuf(ps_, "XcT", [128, 4, 528], BF16)
                kcT = sbuf(ps_, "kcT", [128, 128], BF16)
                VCext = sbuf(ps_, "VCext", [128, 2, 97], BF16)
                maskc = sbuf(ps_, "maskc", [128, 128], BF16)
                maske = sbuf(ps_, "maske", [128, 128], BF16)
                cmpm = sbuf(ps_, "cmpm", [128, NT, 128], BF16)
                fct = sbuf(ps_, "fct", [128, NT, 2, 32], BF16)
                invf = sbuf(ps_, "invf", [128, 32], F32)
                esink = sbuf(ps_, "esink", [128, 8], F32)
                cs2 = sbuf(ps_, "cs2", [128, NT, 32], F32)
                sn2 = sbuf(ps_, "sn2", [128, NT, 32], F32)

                with ExitStack() as p0:
                    wadab = [sbuf(p0, "wada%d" % i, [128, 8, 512], F32) for i in range(2)]
                    cTs = sbuf(p0, "cTs", [128, 8, NB], F32)
                    scT = sbuf(p0, "scT", [128, 8, NB], F32)
                    badaT = sbuf(p0, "badaT", [128, 24], F32)
                    modT = sbuf(p0, "modT", [128, 24, NB], F32)

                    Sc.dma("sp", lambda e: e.dma_start(out=cTs[:], in_=cT_d), writes=[cTs.d()])
                    Sc.dma("sp", lambda e: e.dma_start(out=badaT[:], in_=badaT_d), writes=[badaT.d()])
                    Sc.op("act", lambda e: e.activation(out=scT[:], in_=cTs[:], func=AF.Silu),
                          reads=[cTs.d()], writes=[scT.d()])
                    wada_v = wada_d.rearrange("(k p) n -> p k n", p=128)
                    pm = pmisc[:, 0:48].rearrange("p (j b) -> p j b", b=NB)
                    for blk in range(6):
                        wb_ = wadab[blk % 2]
                        Sc.dma("sp", lambda e: e.dma_start(out=wb_[:], in_=wada_v[:, :, blk * 512:(blk + 1) * 512]),
                               writes=[wb_.d()])

                        def mm_mod(e):
                            for jj in range(4):
                                j = blk * 4 + jj
                                for kc in range(8):
                                    e.matmul(pm[:, j, :], lhsT=wb_[:, kc, jj * 128:(jj + 1) * 128], rhs=scT[:, kc, :],
                                             start=(kc == 0), stop=(kc == 7))
                        Sc.op("pe", mm_mod, reads=[scT.d(), wb_.d()], writes=[pmisc.d()])
                    Sc.op("dve", lambda e: e.tensor_tensor(
                        out=modT[:], in0=pm, in1=badaT[:].unsqueeze(2).broadcast_to([128, 24, NB]), op=ALU.add),
                        reads=[pmisc.d(), badaT.d()], writes=[modT.d()])
                    Sc.op("dve", lambda e: e.tensor_copy(out=shiftT[:], in_=modT[:, 0:8, :]),
                          reads=[modT.d()], writes=[shiftT.d()])
                    Sc.op("dve", lambda e: e.tensor_scalar(out=onepT[:], in0=modT[:, 8:16, :], scalar1=1.0, scalar2=None,
                                                          op0=ALU.add), reads=[modT.d()], writes=[onepT.d()])
                    Sc.op("dve", lambda e: e.tensor_copy(out=gateT[:], in_=modT[:, 16:24, :]),
                          reads=[modT.d()], writes=[gateT.d()])
                    winP_v = winP_d.rearrange("(k p) n -> p k n", p=128)
                    for (c0, c1) in ((1024, 1536), (1536, 2072), (0, 512), (512, 1024)):
                        for kc in range(8):
                            Sc.dma("pool", lambda e: e.dma_start(
                                out=winP[:, kc, c0:c1], in_=winP_v[:, kc, c0:c1], max_dma_last_dim=2048),
                                writes=[winP.d((c0, kc))])
                    for i, (src, dst) in enumerate(((w1k_d, W1[0]), (w1v_d, W1[1]))):
                        for h in range(2):
                            Sc.dma("pool", lambda e: e.dma_start(
                                out=dst[:, h * 8:(h + 1) * 8, :], in_=src[:, h * 8:(h + 1) * 8, :], max_dma_last_dim=2048),
                                writes=[dst.d(h)])
                    for src, dst in ((posk_d, posr[0]), (posv_d, posr[1]), (w2k_d, W2k), (w2v_d, W2v),
                                     (maskc_d, maskc), (maske_d, maske), (cmpm_d, cmpm),
                                     (vcinit_d, VCext), (fc_d, fct)):
                        Sc.dma("pool", lambda e: e.dma_start(out=dst[:], in_=src, max_dma_last_dim=2048), writes=[dst.d()])
                    Sc.op("pool", lambda e: e.memset(KTs[:], 0.0), writes=[KTs.d("w")])
                    Sc.dma("pool", lambda e: e.dma_start(out=KTs[64:96, 0, :], in_=wst_d, max_dma_last_dim=2048), writes=[KTs.d("w")])
                    Sc.dma("pool", lambda e: e.dma_start(out=KTs[0:32, 1, :], in_=wst_d, max_dma_last_dim=2048), writes=[KTs.d("w")])
                    Sc.dma("sp", lambda e: e.dma_start(out=invf[:], in_=invf_d), writes=[invf.d()])
                    Sc.dma("sp", lambda e: e.dma_start(out=esink[:], in_=sinks_d.partition_broadcast(128)),
                           writes=[esink.d()])
                    Sc.op("act", lambda e: e.activation(out=esink[:], in_=esink[:], func=AF.Exp),
                          reads=[esink.d()], writes=[esink.d()])
                    dump("onepT", onepT[:], onepT.d(), [128, 8, NB])
                    dump("shiftT", shiftT[:], shiftT.d(), [128, 8, NB])
                    barrier()
                    cp("p0")

                pbias = pmisc[:, 48:52].rearrange("p (a b) -> p a b", b=2)

                def mm_bias(e):
                    ins = None
                    for kv in range(2):
                        for fc in range(2):
                            for l2 in range(16):
                                ins = e.matmul(pbias[:, kv, fc:fc + 1], lhsT=W1[kv][:, l2, fc * 128:(fc + 1) * 128],
                                               rhs=posr[kv][:, l2:l2 + 1], start=(l2 == 0), stop=(l2 == 15))
                    return ins
                Sc.op("pe", mm_bias, reads=[W1[0].d(0), W1[0].d(1), W1[1].d(0), W1[1].d(1), posr[0].d(), posr[1].d()],
                      writes=[pmisc.d("bias")])
                Sc.op("dve", lambda e: e.tensor_copy(out=biasT[:], in_=pbias), reads=[pmisc.d("bias")], writes=[biasT.d()])
                Sc.op("dve", lambda e: e.tensor_scalar(out=nbiasT[:], in0=pbias, scalar1=-1.0, scalar2=None, op0=ALU.mult),
                      reads=[pmisc.d("bias")], writes=[nbiasT.d()])
                dump("biasT", biasT[:], biasT.d(), [128, 2, 2])
                cp("bias")

                with ExitStack() as pw:
                    xt = [sbuf(pw, "xt%d" % i, [128, D], F32) for i in range(2)]
                    uT = sbuf(pw, "uT", [128, 8, 384], BF16)
                    st6 = sbuf(pw, "st6", [128, 2, 6], F32)
                    mv = sbuf(pw, "mv", [128, 2], F32)
                    rstd = sbuf(pw, "rstd", [128, 1], F32)
                    nmr = sbuf(pw, "nmr", [128, 1], F32)
                    ra = sbuf(pw, "ra", [128, 512], F32)
                    rb = sbuf(pw, "rb", [128, 512], F32)
                    otmp = ra
                    qr = [sbuf(pw, "qr%d" % i, [128, 512], BF16) for i in range(2)]
                    qr.append(sbuf(pw, "qr2", [128, 512], BF16))
                    xdup = sbuf(pw, "xdup", [128, 4, 128], BF16)
                    QaT = sbuf(pw, "QaT", [128, 2, 4, 512], BF16)
                    QbT = sbuf(pw, "QbT", [128, 2, 4, 512], BF16)
                    PT = [sbuf(pw, "PT%d" % i, [128, 512], BF16) for i in range(4)]
                    hidT = sbuf(pw, "hidT", [128, 2, 2, 64], BF16)
                    sg = sbuf(pw, "sg", [128, 4, 24], F32)
                    oacc = sbuf(pw, "oacc", [128, 1024], F32)
                    rs = sbuf(pw, "rs", [128, 4], F32)
                    coef = sbuf(pw, "coef", [128, 4], F32)
                    imp = sbuf(pw, "imp", [128, 32], F32)
                    m8 = sbuf(pw, "m8", [128, 8], F32)
                    nsb = [sbuf(pw, "ns%d" % i, [128, 128], BF16) for i in range(2)]
                    Qsel = [sbuf(pw, "Qsel%d" % i, [128, 4, 128], BF16) for i in range(2)]
                    posf = sbuf(pw, "posf", [128, NT], F32)
                    posi = sbuf(pw, "posi", [128, NB, NT], I32)

                    Sc.dma("sp", lambda e: e.dma_start(out=posi[:], in_=pos_d), writes=[posi.d()])
                    Sc.op("pool", lambda e: e.memset(kcT[:], 0.0), writes=[kcT.d()])
                    Sc.op("pool", lambda e: e.memset(QaT[:], 0.0), writes=[QaT.d(i) for i in range(4)])
                    Sc.op("pool", lambda e: e.memset(QbT[:], 0.0), writes=[QbT.d(i) for i in range(4)])
                    for nb_ in nsb:
                        Sc.op("pool", lambda e: e.memset(nb_[:], 0.0), writes=[nb_.d()])
                    Sc.op("pool", lambda e: e.memset(XcT[:], 0.0), writes=[XcT.d()])
                    Sc.op("pool", lambda e: e.memset(VC[:], 1.0), writes=[VC.d()])

                    pst_i = [0]
                    pt_i = [0]
                    pend = []
                    qs_i = [0]
                    acc_i = [0]
                    a0_done = set()

                    for b in range(NB):
                        Sc.op("dve", lambda e, b=b: e.tensor_copy(out=posf[:], in_=posi[:, b, :]),
                              reads=[posi.d()], writes=[posf.d()])
                        angv = ra[:].rearrange("p (t j) -> p t j", j=32)
                        ang2v = rb[:].rearrange("p (t j) -> p t j", j=32)
                        kkv = oacc[:, 0:512].rearrange("p (t j) -> p t j", j=32)
                        Sc.op("dve", lambda e: e.tensor_tensor(
                            out=angv, in0=posf[:].unsqueeze(2).broadcast_to([128, NT, 32]),
                            in1=invf[:].unsqueeze(1).broadcast_to([128, NT, 32]), op=ALU.mult),
                            reads=[posf.d(), invf.d()], writes=[ra.d()])

                        def reduce_angle(src, sdep, dst, ddep):
                            Sc.op("dve", lambda e: e.tensor_scalar(out=kkv, in0=src, scalar1=1.0 / TWO_PI,
                                                                  scalar2=MAGIC, op0=ALU.mult, op1=ALU.add),
                                  reads=[sdep], writes=[oacc.d()])
                            Sc.op("dve", lambda e: e.tensor_scalar(out=kkv, in0=kkv, scalar1=-MAGIC, scalar2=None,
                                                                  op0=ALU.add), reads=[oacc.d()], writes=[oacc.d()])
                            Sc.op("dve", lambda e: e.scalar_tensor_tensor(out=dst, in0=kkv, scalar=-C1, in1=src,
                                                                         op0=ALU.mult, op1=ALU.add),
                                  reads=[oacc.d(), sdep], writes=[ddep])
                            Sc.op("dve", lambda e: e.scalar_tensor_tensor(out=dst, in0=kkv, scalar=-C2, in1=dst,
                                                                         op0=ALU.mult, op1=ALU.add),
                                  reads=[oacc.d(), ddep], writes=[ddep])
                            Sc.op("dve", lambda e: e.tensor_scalar(out=dst, in0=dst, scalar1=-PI_LO, scalar2=PI_LO,
                                                                  op0=ALU.max, op1=ALU.min), reads=[ddep], writes=[ddep])
                        Sc.op("dve", lambda e: e.tensor_scalar(out=ang2v, in0=angv, scalar1=0.5 * np.pi, scalar2=None,
                                                              op0=ALU.add), reads=[ra.d()], writes=[rb.d()])
                        reduce_angle(ang2v, rb.d(), ang2v, rb.d())
                        Sc.op("act", lambda e: e.activation(out=cs2[:], in_=ang2v, func=AF.Sin),
                              reads=[rb.d()], writes=[cs2.d()])
                        reduce_angle(angv, ra.d(), angv, ra.d())
                        Sc.op("act", lambda e: e.activation(out=sn2[:], in_=angv, func=AF.Sin),
                              reads=[ra.d()], writes=[sn2.d()])
                        if b == 0:
                            dump("cs2", cs2[:], cs2.d(), [128, NT, 32])
                            dump("sn2", sn2[:], sn2.d(), [128, NT, 32])
                        cp("tab")
                        if b > 0:
                            Sc.op("pool", lambda e: e.memset(XcT[:, :, 0:16], 0.0), reads=[], writes=[XcT.d()])

                        for s in range(NST):
                            if s > 0:
                                Sc.op("dve", lambda e: e.tensor_copy(out=XcT[0:64, :, 0:16], in_=XcT[0:64, :, 512:528]),
                                      reads=[XcT.d()], writes=[XcT.d()])
                                Sc.op("dve", lambda e: e.tensor_copy(out=XcT[64:128, :, 0:15], in_=XcT[64:128, :, 512:527]),
                                      reads=[XcT.d()], writes=[XcT.d()])
                            tpv = ptp[:].rearrange("p (a t) -> p a t", t=128)

                            def proj(pb, c0, c1, tl):
                                us_ = (4 * s + tl) % 3

                                def f(e):
                                    for kc in range(8):
                                        e.matmul(pb[:, 0:c1 - c0], lhsT=uT[:, kc, us_ * 128:(us_ + 1) * 128],
                                                 rhs=winP[:, kc, c0:c1], start=(kc == 0), stop=(kc == 7))
                                blk = (0, 512, 1024, 1536)[min(c0 // 512, 3)]
                                Sc.op("pe", f, reads=[uT.d(us_)] + [winP.d((blk, k)) for k in range(8)], writes=[pb.d()])

                            def rope(pb, dst, nh, tg):
                                v4 = lambda ap: ap.rearrange("p (h t j) -> p h t j", t=2, j=32)
                                cosb = cs2[:, tg, :].unsqueeze(1).unsqueeze(1).broadcast_to([128, nh, 2, 32])
                                sinb = sn2[:, tg, :].unsqueeze(1).unsqueeze(1).broadcast_to([128, nh, 2, 32])
                                W_ = nh * 64
                                pbv = pb[:, 0:W_]
                                swp = _mkap(pbv, 32, [[64, nh], [-32, 2], [1, 32]])
                                Sc.op("dve", lambda e: e.tensor_tensor(out=v4(ra[:, 0:W_]), in0=v4(pbv), in1=cosb, op=ALU.mult),
                                      reads=[pb.d(), cs2.d()], writes=[ra.d()])
                                Sc.op("dve", lambda e: e.tensor_tensor(out=v4(rb[:, 0:W_]), in0=swp, in1=sinb, op=ALU.mult),
                                      reads=[pb.d(), sn2.d()], writes=[rb.d()])
                                Sc.op("pool", lambda e: e.tensor_tensor(out=v4(dst[:, 0:W_])[:, :, 0, :], in0=v4(ra[:, 0:W_])[:, :, 0, :],
                                                                       in1=v4(rb[:, 0:W_])[:, :, 0, :], op=ALU.subtract),
                                      reads=[ra.d(), rb.d()], writes=[dst.d()])
                                Sc.op("pool", lambda e: e.tensor_tensor(out=v4(dst[:, 0:W_])[:, :, 1, :], in0=v4(ra[:, 0:W_])[:, :, 1, :],
                                                                       in1=v4(rb[:, 0:W_])[:, :, 1, :], op=ALU.add),
                                      reads=[ra.d(), rb.d()], writes=[dst.d()])

                            def stA0(tl, s_=None):
                                s__ = s if s_ is None else s_
                                tg = 4 * s__ + tl
                                if (b, tg) in a0_done:
                                    return
                                a0_done.add((b, tg))
                                row0 = (b * NT + tg) * 128
                                xb = xt[tg % 2]
                                Sc.dma("sp", lambda e: e.dma_start(out=xb[:, :], in_=x_d[row0:row0 + 128, :]), writes=[xb.d()])
                                ln_stats(xb, xb.d(), st6, mv[:], rstd[:], nmr[:], (st6.d(), mv.d(), rstd.d(), nmr.d()))
                                Sc.op("dve", lambda e: e.tensor_scalar(out=xb[:, :], in0=xb[:, :], scalar1=rstd[:], scalar2=nmr[:],
                                                                      op0=ALU.mult, op1=ALU.add),
                                      reads=[xb.d(), rstd.d(), nmr.d()], writes=[xb.d()])

                            def stA(tl):
                                tg = 4 * s + tl
                                xb = xt[tg % 2]
                                us_ = tg % 3
                                for half in range(2):
                                    pb = pj[half]

                                    def tr(e):
                                        for i in range(4):
                                            kc = half * 4 + i
                                            e.transpose(out=pb[:, i * 128:(i + 1) * 128], in_=xb[:, kc * 128:(kc + 1) * 128],
                                                        identity=identf[:])
                                    Sc.op("pe", tr, reads=[xb.d(), identf.d()], writes=[pb.d()])
                                    for i in range(4):
                                        kc = half * 4 + i
                                        Sc.op("act", lambda e: e.activation(
                                            out=uT[:, kc, us_ * 128:(us_ + 1) * 128], in_=pb[:, i * 128:(i + 1) * 128], func=AF.Identity,
                                            scale=onepT[:, kc, b:b + 1], bias=shiftT[:, kc, b:b + 1]),
                                            reads=[pb.d(), onepT.d(), shiftT.d()], writes=[uT.d(us_)])

                            def stB(tl):
                                tg = 4 * s + tl
                                proj(pst[0], 0, 512, tl)
                                rope(pst[0], qr[0], 8, tg)
                                proj(pst[1], 512, 1024, tl)
                                rope(pst[1], qr[1], 8, tg)

                            def stB2(tl):
                                tg = 4 * s + tl

                                def tr_q(e):
                                    for qi, src in enumerate((qr[0], qr[1])):
                                        for hl in range(4):
                                            e.transpose(out=tpv[:, qi * 4 + hl, :], in_=src[:, hl * 128:(hl + 1) * 128], identity=ident[:])
                                Sc.op("pe", tr_q, reads=[qr[0].d(), qr[1].d(), ident.d()], writes=[ptp.d()])
                                for g_ in range(2):
                                    gs_ = slice(g_ * 64, (g_ + 1) * 64)
                                    Sc.op("dve", lambda e: e.tensor_copy(out=QaT[gs_, g_, :, tl * 128:(tl + 1) * 128], in_=tpv[gs_, 0:4, :]),
                                          reads=[ptp.d()], writes=[QaT.d(tl)])
                                    Sc.op("dve", lambda e: e.tensor_copy(out=QbT[gs_, g_, :, tl * 128:(tl + 1) * 128], in_=tpv[gs_, 4:8, :]),
                                          reads=[ptp.d()], writes=[QbT.d(tl)])

                            def stC(tl):
                                tg = 4 * s + tl
                                proj(po[0], 1024, 1536, tl)
                                rope(po[0], qr[2], 8, tg)
                                proj(po[1], 1536, 2048, tl)
                                Sc.op("act", lambda e: e.copy(
                                    out=xdup[:, 2:4, :].rearrange("p g (r d) -> p g r d", r=2),
                                    in_=po[1][:, 0:128].rearrange("p (g d) -> p g d", g=2).unsqueeze(2).broadcast_to([128, 2, 2, 64])),
                                    reads=[po[1].d()], writes=[xdup.d("v")])
                                Sc.op("pool", lambda e: e.tensor_copy(
                                    out=xdup[:, 0:2, :].rearrange("p g (r d) -> p g r d", r=2),
                                    in_=qr[2][:, 0:128].rearrange("p (g d) -> p g d", g=2).unsqueeze(2).broadcast_to([128, 2, 2, 64])),
                                    reads=[qr[2].d()], writes=[xdup.d("k")])
                                Sc.op("act", lambda e: e.copy(
                                    out=VC[:, tg, :, :, 0:64],
                                    in_=po[1][:, 128:512].rearrange("p (k g d) -> p k g d", k=3, g=2)),
                                    reads=[po[1].d()], writes=[VC.d(tg)])
                                pg = pmisc[:, 64:88]

                                us_ = tg % 3

                                def mm_g(e):
                                    for kc in range(8):
                                        e.matmul(pg, lhsT=uT[:, kc, us_ * 128:(us_ + 1) * 128], rhs=winP[:, kc, 2048:2072],
                                                 start=(kc == 0), stop=(kc == 7))
                                Sc.op("pe", mm_g, reads=[uT.d(us_)] + [winP.d((1536, k)) for k in range(8)], writes=[pmisc.d("g")])
                                Sc.op("act", lambda e: e.activation(out=sg[:, tl, :], in_=pg, func=AF.Exp, scale=-1.0),
                                      reads=[pmisc.d("g")], writes=[sg.d(tl)])
                                Sc.op("dve", lambda e: e.tensor_scalar(out=sg[:, tl, :], in0=sg[:, tl, :], scalar1=1.0, scalar2=None,
                                                                      op0=ALU.add), reads=[sg.d(tl)], writes=[sg.d(tl)])
                                Sc.op("dve", lambda e: e.reciprocal(out=sg[:, tl, :], in_=sg[:, tl, :]), reads=[sg.d(tl)], writes=[sg.d(tl)])

                            def stC2(tl):
                                tg = 4 * s + tl

                                def tr_k(e):
                                    for i4 in range(4):
                                        e.transpose(out=tpv[:, i4, :], in_=xdup[:, i4, :], identity=ident[:])
                                    for k in range(3):
                                        e.transpose(out=tpv[:, 4 + k, :], in_=qr[2][:, 128 * (k + 1):128 * (k + 2)], identity=ident[:])
                                Sc.op("pe", tr_k, reads=[qr[2].d(), xdup.d("k"), xdup.d("v"), ident.d()], writes=[ptp.d()])
                                Sc.op("dve", lambda e: e.tensor_copy(out=KT[:, :, tg * 128:(tg + 1) * 128], in_=tpv[:, 5:7, :]),
                                      reads=[ptp.d()], writes=[KT.d(tg)])
                                Sc.op("dve", lambda e: e.tensor_copy(out=KTs[0:64, 0, tg * 128:(tg + 1) * 128], in_=tpv[0:64, 4, :]),
                                      reads=[ptp.d()], writes=[KTs.d(tg)])
                                Sc.op("dve", lambda e: e.tensor_copy(out=KTs[64:128, 1, tg * 128:(tg + 1) * 128], in_=tpv[64:128, 4, :]),
                                      reads=[ptp.d()], writes=[KTs.d(tg)])
                                Sc.op("dve", lambda e: e.tensor_copy(
                                    out=XcT[0:64, :, 16 + tl * 128:16 + (tl + 1) * 128], in_=tpv[0:64, 0:4, :]),
                                    reads=[ptp.d()], writes=[XcT.d()])
                                Sc.op("dve", lambda e: e.tensor_copy(
                                    out=XcT[64:128, :, 15 + tl * 128:15 + (tl + 1) * 128], in_=tpv[64:128, 0:4, :]),
                                    reads=[ptp.d()], writes=[XcT.d()])

                            stA0(0)
                            for step in range(4 + 1):
                                ok = lambda t_: 0 <= t_ < 4
                                if ok(step - 1):
                                    stC(step - 1)
                                    stB(step - 1)
                                if ok(step):
                                    stA(step)
                                if ok(step - 1):
                                    stC2(step - 1)
                                    stB2(step - 1)
                                if step + 1 < 4:
                                    stA0(step + 1)

                            cp("s2")
                            phs = [pmisc[:, 128 + 128 * i:256 + 128 * i].rearrange("p (a c) -> p a c", a=2) for i in range(2)]
                            for kv in range(2):
                                def mm_h(e):
                                    for fc in range(2):
                                        for l2 in range(16):
                                            rhs = _mkap(XcT[:, 2 * kv, :], 2 * l2, [[528, 2], [16, 32]])
                                            e.matmul(phs[kv][:, fc, :].rearrange("p (g c) -> p g c", g=2),
                                                     lhsT=W1[kv][:, l2, fc * 128:(fc + 1) * 128],
                                                     rhs=rhs, start=(l2 == 0), stop=(l2 == 15))
                                Sc.op("pe", mm_h, reads=[XcT.d(), W1[kv].d(0), W1[kv].d(1)], writes=[pmisc.d(("h", kv))])
                            for kv in range(2):
                                ph = phs[kv]
                                hx = rb[:, 0:256].rearrange("p (k a c) -> p k a c", k=2, a=2)[:, kv, :, :]
                                hd = hidT[:, kv, :, :]
                                for fc in range(2):
                                    Sc.op("act", lambda e: e.activation(
                                        out=hx[:, fc, :], in_=ph[:, fc, :], func=AF.Exp, bias=nbiasT[:, kv, fc:fc + 1], scale=-1.0),
                                        reads=[pmisc.d(("h", kv)), nbiasT.d()], writes=[rb.d()])
                                Sc.op("dve", lambda e: e.tensor_scalar(out=hx, in0=hx, scalar1=1.0, scalar2=None, op0=ALU.add),
                                      reads=[rb.d()], writes=[rb.d()])
                                Sc.op("dve", lambda e: e.reciprocal(out=hx, in_=hx), reads=[rb.d()], writes=[rb.d()])
                                for fc in range(2):
                                    Sc.op("dve", lambda e: e.scalar_tensor_tensor(
                                        out=hd[:, fc, :], in0=ph[:, fc, :], scalar=biasT[:, kv, fc:fc + 1], in1=hx[:, fc, :],
                                        op0=ALU.add, op1=ALU.mult), reads=[pmisc.d(("h", kv)), biasT.d(), rb.d()], writes=[hidT.d(kv)])
                            pk = pmisc[:, 384:448]

                            def mm_k(e):
                                for fc in range(2):
                                    e.matmul(pk, lhsT=W2k[:, fc, :], rhs=hidT[:, 0, fc, :], start=(fc == 0), stop=(fc == 1))
                            Sc.op("pe", mm_k, reads=[hidT.d(0), W2k.d()], writes=[pmisc.d("k")])
                            for g in range(2):
                                Sc.op("dve", lambda e: e.tensor_copy(
                                    out=kcT[g * 64:(g + 1) * 64, 32 * s:32 * s + 32], in_=pk[g * 64:(g + 1) * 64, g * 32:(g + 1) * 32]),
                                    reads=[pmisc.d("k")], writes=[kcT.d()])
                            for g in range(2):
                                pv = pmisc[0:32, 0:64] if g == 0 else pmisc[0:32, 448:512]

                                def mm_v(e):
                                    for fc in range(2):
                                        e.matmul(pv, lhsT=hidT[:, 1, fc, g * 32:(g + 1) * 32], rhs=W2v[:, fc, :],
                                                 start=(fc == 0), stop=(fc == 1))
                                Sc.op("pe", mm_v, reads=[hidT.d(1), W2v.d()], writes=[pmisc.d(("v", g))])
                                Sc.op("dve", lambda e: e.tensor_copy(out=VCext[32 * s:32 * s + 32, g, 0:64], in_=pv),
                                      reads=[pmisc.d(("v", g))], writes=[VCext.d()])
                            if b == 0 and s == 0:
                                dump("kcT0", kcT[:], kcT.d(), [128, 128], BF16)
                                dump("VCext0", VCext[:], VCext.d(), [128, 2, 97], BF16)

                            cp("s2b")
                            if s + 1 < NST:
                                stA0(0, s + 1)
                            for tl in range(4):
                                tg = 4 * s + tl
                                for g in range(2):
                                    gsl = slice(g * 64, (g + 1) * 64)
                                    qa_rhs = QaT[:, g, :, tl * 128:(tl + 1) * 128]
                                    qb_rhs = QbT[:, g, :, tl * 128:(tl + 1) * 128]

                                    def pair(lhsT_k, k_deps, q_rhs, q_dep, mask_kind, j, vrhs, v_deps, pacc, ncol, first, last,
                                             post=(), pre=()):
                                        for f in pre:
                                            f()
                                        stb = (pst[0], pst[1], pj[0], pmisc)[pst_i[0] % 4]
                                        pst_i[0] += 1
                                        ptb = PT[pt_i[0] % 4]
                                        pt_i[0] += 1

                                        def mm_s(e):
                                            e.matmul(stb[:].rearrange("p (h q) -> p h q", h=4), lhsT=lhsT_k, rhs=q_rhs,
                                                     start=True, stop=True)
                                        rd = list(k_deps) + [q_dep]
                                        Sc.op("pe", mm_s, reads=rd, writes=[stb.d()])
                                        Sc.op("act", lambda e: e.activation(out=ptb[:], in_=stb[:], func=AF.Exp, scale=0.125),
                                              reads=[stb.d()], writes=[ptb.d()])
                                        if mask_kind is not None:
                                            p3 = ptb[:].rearrange("p (h q) -> p h q", h=4)
                                            Sc.op("dve", lambda e: e.tensor_tensor(
                                                out=p3, in0=p3, in1=mask_kind.unsqueeze(1).broadcast_to([128, 4, 128]), op=ALU.mult),
                                                reads=[ptb.d(), maskc.d(), maske.d(), cmpm.d()], writes=[ptb.d()])

                                        def mm_pv(e):
                                            ins = None
                                            for hl in range(4):
                                                ins = e.matmul(pacc[:, hl * ncol:(hl + 1) * ncol], lhsT=ptb[:, hl * 128:(hl + 1) * 128],
                                                               rhs=vrhs, start=(first and hl == 0), stop=last,
                                                               skip_group_check=True)
                                            return ins

                                        def do_pv():
                                            Sc.op("pe", mm_pv, reads=[ptb.d()] + list(v_deps), writes=[pacc.d()])
                                            for f in post:
                                                f()
                                        while len(pend) >= 3:
                                            pend.pop(0)()
                                        pend.append(do_pv)

                                    def finish_branch(pacc, ncol, sgcol, dst_ap, accumulate, kind, g=g, tl=tl):
                                        pv3 = pacc[:, 0:4 * ncol].rearrange("p (h c) -> p h c", h=4)
                                        rsum = pv3[:, :, 64:65].rearrange("p h c -> p (h c)")
                                        if kind == "cmp":
                                            Sc.op("dve", lambda e: e.tensor_scalar(out=rs[:], in0=rsum, scalar1=1e-30, scalar2=None,
                                                                                  op0=ALU.max), reads=[pacc.d()], writes=[rs.d()])
                                        elif kind == "swa":
                                            Sc.op("dve", lambda e: e.tensor_tensor(out=rs[:], in0=rsum, in1=esink[:, g * 4:(g + 1) * 4],
                                                                                  op=ALU.add),
                                                  reads=[pacc.d(), esink.d()], writes=[rs.d()])
                                        else:
                                            Sc.op("dve", lambda e: e.tensor_copy(out=rs[:], in_=rsum), reads=[pacc.d()], writes=[rs.d()])
                                        Sc.op("dve", lambda e: e.reciprocal(out=rs[:], in_=rs[:]), reads=[rs.d()], writes=[rs.d()])
                                        if sgcol is not None:
                                            sgv = _mkap(sg[:, tl, :], g * 12 + sgcol, [[3, 4]])
                                            Sc.op("dve", lambda e: e.tensor_tensor(out=coef[:], in0=rs[:], in1=sgv, op=ALU.mult),
                                                  reads=[rs.d(), sg.d(tl)], writes=[coef.d()])
                                            cf = coef
                                        else:
                                            cf = rs
                                        cb = cf[:].unsqueeze(2).broadcast_to([128, 4, 64])
                                        if not accumulate:
                                            Sc.op("dve", lambda e: e.tensor_tensor(
                                                out=dst_ap.rearrange("p (h d) -> p h d", h=4), in0=pv3[:, :, 0:64], in1=cb, op=ALU.mult),
                                                reads=[pacc.d(), cf.d()], writes=[oacc.d()])
                                        else:
                                            Sc.op("dve", lambda e: e.tensor_tensor(
                                                out=otmp[:, 0:256].rearrange("p (h d) -> p h d", h=4), in0=pv3[:, :, 0:64], in1=cb, op=ALU.mult),
                                                reads=[pacc.d(), cf.d()], writes=[otmp.d()])
                                            Sc.op("pool", lambda e: e.tensor_tensor(out=dst_ap, in0=dst_ap, in1=otmp[:, 0:256], op=ALU.add),
                                                  reads=[otmp.d(), oacc.d()], writes=[oacc.d()])

                                    oa_dst = oacc[:, g * 256:(g + 1) * 256]
                                    ob_dst = oacc[:, 512 + g * 256:512 + (g + 1) * 256]

                                    accs = []
                                    for _ in range(4):
                                        accs.append((po[0], po[1], pj[1])[acc_i[0] % 3])
                                        acc_i[0] += 1
                                    acc_cmp, acc_win, acc_swa, acc_sel = accs
                                    nsg = nsb[g]
                                    nc0 = 64 if g == 0 else 0
                                    qsb = Qsel[qs_i[0] % 2]
                                    qs_i[0] += 1

                                    def select_blocks(tg=tg, nsg=nsg, nc0=nc0, acc_cmp=acc_cmp):
                                        pc3 = acc_cmp[:, 0:388].rearrange("p (h c) -> p h c", h=4)
                                        Sc.op("dve", lambda e: e.tensor_scalar(
                                            out=rs[:], in0=pc3[:, :, 64:65].rearrange("p h c -> p (h c)"), scalar1=1e-30, scalar2=None,
                                            op0=ALU.max), reads=[acc_cmp.d()], writes=[rs.d()])
                                        Sc.op("dve", lambda e: e.reciprocal(out=coef[:], in_=rs[:]), reads=[rs.d()], writes=[coef.d()])
                                        Sc.op("dve", lambda e: e.tensor_scalar(
                                            out=imp[:], in0=pc3[:, 0, 65:97], scalar1=coef[:, 0:1], scalar2=None, op0=ALU.mult),
                                            reads=[acc_cmp.d(), coef.d()], writes=[imp.d()])
                                        for hl in range(1, 4):
                                            Sc.op("dve", lambda e: e.scalar_tensor_tensor(
                                                out=imp[:], in0=pc3[:, hl, 65:97], scalar=coef[:, hl:hl + 1], in1=imp[:],
                                                op0=ALU.mult, op1=ALU.add), reads=[acc_cmp.d(), coef.d(), imp.d()], writes=[imp.d()])
                                        Sc.op("dve", lambda e: e.tensor_tensor(out=imp[:], in0=imp[:], in1=fct[:, tg, 0, :], op=ALU.max),
                                              reads=[imp.d(), fct.d()], writes=[imp.d()])
                                        Sc.op("dve", lambda e: e.tensor_tensor(out=imp[:], in0=imp[:], in1=fct[:, tg, 1, :], op=ALU.add),
                                              reads=[imp.d(), fct.d()], writes=[imp.d()])
                                        Sc.op("dve", lambda e: e.max(out=m8[:], in_=imp[:]), reads=[imp.d()], writes=[m8.d()])
                                        Sc.op("dve", lambda e: e.tensor_scalar(out=nsg[:, nc0:nc0 + 32], in0=imp[:], scalar1=m8[:, 7:8], scalar2=-1.0,
                                                                              op0=ALU.is_ge, op1=ALU.add),
                                              reads=[imp.d(), m8.d()], writes=[nsg.d()])

                                    def transpose_ns(nsg=nsg, nc0=nc0, qsb=qsb, qa_rhs=qa_rhs, tl=tl):
                                        ptn = ptp[:, 896:1024]
                                        Sc.op("dve", lambda e: e.tensor_copy(out=qsb[:], in_=qa_rhs), reads=[QaT.d(tl)], writes=[qsb.d()])
                                        Sc.op("pe", lambda e: e.transpose(out=ptn, in_=nsg[:], identity=ident[:]),
                                              reads=[nsg.d(), ident.d()], writes=[ptp.d("ns")])
                                        Sc.op("act", lambda e: e.copy(out=qsb[nc0:nc0 + 32, :, :],
                                                                      in_=ptn[nc0:nc0 + 32, :].unsqueeze(1).broadcast_to([32, 4, 128])),
                                              reads=[ptp.d("ns")], writes=[qsb.d()])

                                    pair(kcT[:, :], [kcT.d()], qa_rhs, QaT.d(tl), cmpm[:, tg, :], 0,
                                         VCext[:, g, :], [VCext.d()], acc_cmp, 97, True, True,
                                         post=[select_blocks,
                                               lambda oa_dst=oa_dst, fb=finish_branch, a_=acc_cmp: fb(a_, 97, 0, oa_dst, False, "cmp")])
                                    j0 = max(0, tg - 4)
                                    for j in range(j0, tg + 1):
                                        mk = maskc[:] if j == tg else (maske[:] if j == tg - 4 else None)
                                        pair(KT[:, 0, j * 128:(j + 1) * 128], [KT.d(j)], qa_rhs, QaT.d(tl), mk, j,
                                             VC[:, j, 1, g, :], [VC.d(j), VC.d()], acc_win, 65, j == j0, j == tg,
                                             post=([lambda oa_dst=oa_dst, fb=finish_branch, a_=acc_win: fb(a_, 65, 2, oa_dst, True, "win")] if j == tg else ()))
                                    j0 = max(0, tg - 1)
                                    for j in range(j0, tg + 1):
                                        mk = maskc[:] if j == tg else maske[:]
                                        pair(KT[:, 1, j * 128:(j + 1) * 128], [KT.d(j)], qb_rhs, QbT.d(tl), mk, j,
                                             VC[:, j, 2, g, :], [VC.d(j), VC.d()], acc_swa, 65, j == j0, j == tg,
                                             post=([lambda ob_dst=ob_dst, fb=finish_branch, a_=acc_swa: fb(a_, 65, None, ob_dst, False, "swa")] if j == tg else ()))
                                    for j in range(tg + 1):
                                        pst_ = []
                                        if j == tg:
                                            pst_.append(lambda oa_dst=oa_dst, fb=finish_branch, a_=acc_sel: fb(a_, 65, 1, oa_dst, True, "sel"))
                                            if g == 1:
                                                def fin(b=b, tg=tg):
                                                    Sc.op("act", lambda e: e.copy(out=OA[:, b * NT + tg, :], in_=oacc[:]),
                                                          reads=[oacc.d()], writes=[OA.d(b * NT + tg)])
                                                pst_.append(fin)
                                        pair(KTs[:, g, j * 128:(j + 1) * 128], [KTs.d(j), KTs.d("w")],
                                             qa_rhs if j == tg else qsb[:], QaT.d(tl) if j == tg else qsb.d(),
                                             maskc[:] if j == tg else None, j,
                                             VC[:, j, 0, g, :], [VC.d(j), VC.d()], acc_sel, 65, j == 0, j == tg,
                                             post=pst_, pre=([transpose_ns] if j == 0 else ()))
                            while pend:
                                pend.pop(0)()
                    barrier()
                    cp("P")

            with ExitStack() as qs:
                winQ = sbuf(qs, "winQ", [128, 8, QW], BF16)
                wupa = sbuf(qs, "wupa", [128, 4, D], BF16)
                wupb = sbuf(qs, "wupb", [128, 4, D], BF16)
                woutb = sbuf(qs, "woutb", [128, 8, D], BF16)
                lng = sbuf(qs, "lng", [128, D], F32)
                lnb = sbuf(qs, "lnb", [128, D], F32)
                gate_rep = sbuf(qs, "gate_rep", [128, NB, D], F32)

                winQ_v = winQ_d.rearrange("(k p) n -> p k n", p=128)
                for c0 in range(0, QW, 512):
                    for kc in range(8):
                        Sc.dma("pool", lambda e: e.dma_start(
                            out=winQ[:, kc, c0:c0 + 512], in_=winQ_v[:, kc, c0:c0 + 512], max_dma_last_dim=2048),
                            writes=[winQ.d((c0, kc))])
                for src, dst, nk in ((wupa_d, wupa, 4), (wupb_d, wupb, 4), (wout_d, woutb, 8)):
                    sv = src.rearrange("(k p) n -> p k n", p=128)
                    for kc in range(nk):
                        Sc.dma("pool", lambda e: e.dma_start(out=dst[:, kc, :], in_=sv[:, kc, :], max_dma_last_dim=2048),
                               writes=[dst.d(kc)])
                Sc.dma("sp", lambda e: e.dma_start(out=lng[:], in_=lng_d.partition_broadcast(128)), writes=[lng.d()])
                Sc.dma("sp", lambda e: e.dma_start(out=lnb[:], in_=lnb_d.partition_broadcast(128)), writes=[lnb.d()])

                with ExitStack() as qt:
                    gbc = sbuf(qt, "gbc", [128, 8, NB, 128], F32)
                    Sc.op("dve", lambda e: e.tensor_copy(out=gbc[:], in_=gateT[:].unsqueeze(3).broadcast_to([128, 8, NB, 128])),
                          reads=[gateT.d()], writes=[gbc.d()])
                    for b in range(NB):
                        for half in range(2):
                            pb = pj[half]

                            def mm_gr(e):
                                for i in range(4):
                                    e.matmul(pb[:, i * 128:(i + 1) * 128], lhsT=gbc[:, half * 4 + i, b, :], rhs=identf[:],
                                             start=True, stop=True)
                            Sc.op("pe", mm_gr, reads=[gbc.d(), identf.d()], writes=[pb.d()])
                            Sc.op("act", lambda e: e.copy(out=gate_rep[:, b, half * 512:(half + 1) * 512], in_=pb[:]),
                                  reads=[pb.d()], writes=[gate_rep.d()])
                    barrier()

                NXQ = 3
                xq = [sbuf(qs, "xq%d" % i, [128, D], F32) for i in range(NXQ)]
                xn = sbuf(qs, "xn", [128, D], F32)
                uTq = [sbuf(qs, "uTq%d" % i, [128, 8, 128], BF16) for i in range(2)]
                st6 = sbuf(qs, "st6q", [128, 2, 6], F32)
                mv = sbuf(qs, "mvq", [128, 2], F32)
                rstd = sbuf(qs, "rstdq", [128, 1], F32)
                nmr = sbuf(qs, "nmrq", [128, 1], F32)
                st6b = sbuf(qs, "st6qb", [128, 2, 6], F32)
                mvb = sbuf(qs, "mvqb", [128, 2], F32)
                rstdb = sbuf(qs, "rstdqb", [128, 1], F32)
                nmrb = sbuf(qs, "nmrqb", [128, 1], F32)
                szz = sbuf(qs, "szz", [128, 1024], BF16)
                sgm = sbuf(qs, "sgm", [128, 2048], BF16)
                yab = sbuf(qs, "yab", [128, 1024], BF16)
                yT = [sbuf(qs, "yT%d" % i, [128, 8, 128], BF16) for i in range(2)]
                m1 = sbuf(qs, "m1", [128, D], F32)
                m2 = sbuf(qs, "m2", [128, 512], F32)
                mg = sbuf(qs, "mg", [128, D], BF16)
                mT = [sbuf(qs, "mT%d" % i, [128, 8, 128], BF16) for i in range(2)]
                tt = sbuf(qs, "tt", [128, 512], F32)

                tpv = ptp[:].rearrange("p (a t) -> p a t", t=128)
                obanks = [pst[0], pst[1], po[0], po[1]]
                ob_i = [0]

                def nextbank():
                    bk = obanks[ob_i[0] % 4]
                    ob_i[0] += 1
                    return bk

                NTQ = NB * NT

                def Q1(ti):
                    b = ti // NT
                    row0 = ti * 128
                    xb = xq[ti % NXQ]
                    ub = uTq[ti % 2]
                    Sc.dma("sp", lambda e: e.dma_start(out=xb[:, :], in_=x_d[row0:row0 + 128, :]), writes=[xb.d()])
                    ln_stats(xb, xb.d(), st6, mv[:], rstd[:], nmr[:], (st6.d(), mv.d(), rstd.d(), nmr.d()))
                    Sc.op("dve", lambda e: e.tensor_scalar(out=xn[:, :], in0=xb[:, :], scalar1=rstd[:],
                                                          scalar2=nmr[:], op0=ALU.mult, op1=ALU.add),
                          reads=[xb.d(), rstd.d(), nmr.d()], writes=[xn.d()])
                    for half in range(2):
                        pb = pj[half]

                        def tr(e):
                            for i in range(4):
                                kc = half * 4 + i
                                e.transpose(out=pb[:, i * 128:(i + 1) * 128], in_=xn[:, kc * 128:(kc + 1) * 128],
                                            identity=identf[:])
                        Sc.op("pe", tr, reads=[xn.d(), identf.d()], writes=[pb.d()])
                        for i in range(4):
                            kc = half * 4 + i
                            Sc.op("act", lambda e: e.activation(
                                out=ub[:, kc, :], in_=pb[:, i * 128:(i + 1) * 128], func=AF.Identity,
                                scale=onepT[:, kc, b:b + 1], bias=shiftT[:, kc, b:b + 1]),
                                reads=[pb.d(), onepT.d(), shiftT.d()], writes=[ub.d()])

                def Q2(ti):
                    ub = uTq[ti % 2]
                    yt_ = yT[ti % 2]
                    for i in range(2):
                        pb = nextbank()
                        c0 = i * 512

                        def f(e):
                            for kc in range(8):
                                e.matmul(pb[:], lhsT=ub[:, kc, :], rhs=winQ[:, kc, c0:c0 + 512],
                                         start=(kc == 0), stop=(kc == 7))
                        Sc.op("pe", f, reads=[ub.d()] + [winQ.d((c0, k)) for k in range(8)], writes=[pb.d()])
                        Sc.op("act", lambda e: e.activation(out=szz[:, i * 512:(i + 1) * 512], in_=pb[:], func=AF.Silu),
                              reads=[pb.d()], writes=[szz.d(i)])
                    Sc.op("dve", lambda e: e.tensor_tensor(out=yab[:], in0=OA[:, ti, :], in1=szz[:], op=ALU.mult),
                          reads=[OA.d(ti), szz.d(0), szz.d(1)], writes=[yab.d()])

                    def tr_y(e):
                        for i in range(8):
                            e.transpose(out=tpv[:, i, :], in_=yab[:, i * 128:(i + 1) * 128], identity=ident[:])
                    Sc.op("pe", tr_y, reads=[yab.d(), ident.d()], writes=[ptp.d()])
                    Sc.op("act", lambda e: e.copy(out=yt_[:], in_=tpv), reads=[ptp.d()], writes=[yt_.d()])

                def Q3(ti):
                    ub = uTq[ti % 2]
                    yt_ = yT[ti % 2]
                    mt_ = mT[ti % 2]
                    for i in range(4):
                        pb = nextbank()
                        c0 = 1024 + i * 512

                        def f(e):
                            for kc in range(8):
                                e.matmul(pb[:], lhsT=ub[:, kc, :], rhs=winQ[:, kc, c0:c0 + 512],
                                         start=(kc == 0), stop=(kc == 7))
                        Sc.op("pe", f, reads=[ub.d()] + [winQ.d((c0, k)) for k in range(8)], writes=[pb.d()])
                        Sc.op("act", lambda e: e.activation(out=sgm[:, i * 512:(i + 1) * 512], in_=pb[:], func=AF.Sigmoid),
                              reads=[pb.d()], writes=[sgm.d(i)])
                    for n in range(2):
                        for ab, wu in enumerate((wupa, wupb)):
                            pb = nextbank()

                            def f(e):
                                for kc in range(4):
                                    e.matmul(pb[:], lhsT=yt_[:, ab * 4 + kc, :], rhs=wu[:, kc, n * 512:(n + 1) * 512],
                                             start=(kc == 0), stop=(kc == 3))
                            Sc.op("pe", f, reads=[yt_.d()] + [wu.d(k) for k in range(4)], writes=[pb.d()])
                            dst = m1[:, n * 512:(n + 1) * 512] if ab == 0 else m2[:]
                            ddep = m1.d(n) if ab == 0 else m2.d()
                            Sc.op("dve", lambda e: e.tensor_tensor(
                                out=dst, in0=pb[:],
                                in1=sgm[:, ab * 1024 + n * 512:ab * 1024 + (n + 1) * 512], op=ALU.mult),
                                reads=[pb.d(), sgm.d(ab * 2 + n)], writes=[ddep])
                        Sc.op("pool", lambda e: e.tensor_tensor(out=mg[:, n * 512:(n + 1) * 512], in0=m1[:, n * 512:(n + 1) * 512],
                                                               in1=m2[:], op=ALU.add),
                              reads=[m1.d(n), m2.d()], writes=[mg.d(n)])

                    def tr_m(e):
                        for i in range(8):
                            e.transpose(out=tpv[:, i, :], in_=mg[:, i * 128:(i + 1) * 128], identity=ident[:])
                    Sc.op("pe", tr_m, reads=[mg.d(0), mg.d(1), ident.d()], writes=[ptp.d()])
                    Sc.op("act", lambda e: e.copy(out=mt_[:], in_=tpv), reads=[ptp.d()], writes=[mt_.d()])

                def Q4(ti):
                    b = ti // NT
                    row0 = ti * 128
                    xb = xq[ti % NXQ]
                    mt_ = mT[ti % 2]
                    for n in range(2):
                        pb = nextbank()

                        def f(e):
                            for kc in range(8):
                                e.matmul(pb[:], lhsT=mt_[:, kc, :], rhs=woutb[:, kc, n * 512:(n + 1) * 512],
                                         start=(kc == 0), stop=(kc == 7))
                        Sc.op("pe", f, reads=[mt_.d()] + [woutb.d(k) for k in range(8)], writes=[pb.d()])
                        Sc.op("dve", lambda e: e.tensor_tensor(
                            out=tt[:], in0=pb[:], in1=gate_rep[:, b, n * 512:(n + 1) * 512], op=ALU.mult),
                            reads=[pb.d(), gate_rep.d()], writes=[tt.d()])
                        Sc.op("dve", lambda e: e.scalar_tensor_tensor(out=xb[:, n * 512:(n + 1) * 512], in0=xb[:, n * 512:(n + 1) * 512],
                                                                     scalar=ALPHA, in1=tt[:], op0=ALU.mult, op1=ALU.add),
                              reads=[xb.d(), tt.d()], writes=[xb.d()])
                    ln_stats(xb, xb.d(), st6b, mvb[:], rstdb[:], nmrb[:], (st6b.d(), mvb.d(), rstdb.d(), nmrb.d()))
                    Sc.op("act", lambda e: e.activation(out=xb[:, :], in_=xb[:, :], func=AF.Identity, scale=rstdb[:],
                                                       bias=nmrb[:]), reads=[xb.d(), rstdb.d(), nmrb.d()], writes=[xb.d()])
                    Sc.op("pool", lambda e: e.tensor_tensor(out=xb[:, :], in0=xb[:, :], in1=lng[:], op=ALU.mult),
                          reads=[xb.d(), lng.d()], writes=[xb.d()])
                    Sc.op("pool", lambda e: e.tensor_tensor(out=xb[:, :], in0=xb[:, :], in1=lnb[:], op=ALU.add),
                          reads=[xb.d(), lnb.d()], writes=[xb.d()])
                    Sc.dma("sp", lambda e: e.dma_start(out=out_d[row0:row0 + 128, :], in_=xb[:, :]), reads=[xb.d()])

                stages = [Q1, Q2, Q3, Q4]
                for step in range(NTQ + 3):
                    for k in (3, 2, 1, 0):
                        ti = step - k
                        if 0 <= ti < NTQ:
                            stages[k](ti)

        try:
            body()
        except _Stop:
            pass
        Sc.finish()
        with nc.Block() as block:
            Sc.emit(block)
    return nc, dbg_outs


_IN_SPLITS = [512, 256, 256, 256, 24, 512, 512, 256, 512, 2048]


def _constants():
    k = np.arange(128)[:, None]
    q = np.arange(128)[None, :]
    neg = np.float32(-NEGM)
    c = {}
    c["c_ident"] = np.eye(128, dtype=np.float32)
    c["c_maskc"] = np.where(k <= q, 1.0, 0.0).astype(np.float32)
    c["c_maske"] = np.where(k > q, 1.0, 0.0).astype(np.float32)
    cm = np.zeros((128, NT, 128), np.float32)
    for t in range(NT):
        vis = (16 * k + 15 <= 128 * t + q) & (k >= 1)
        cm[:, t, :] = np.where(vis, 1.0, 0.0)
    c["c_cmpm"] = cm
    blk = np.arange(32)[:, None]
    m = np.arange(S)[None, :]
    c["c_wst"] = np.where(m // 64 == blk, np.float32(NEGM), 0.0).astype(np.float32)
    fc = np.zeros((128, NT, 2, 32), np.float32)
    j = np.arange(32)[None, :]
    for t in range(NT):
        cur = (128 * t + np.arange(128)[:, None]) // 64
        forced = (j == 0) | (j == cur) | (j == cur - 1)
        fc[:, t, 0, :] = np.where(forced, 1e9, 0.0)
        fc[:, t, 1, :] = np.where(j <= cur, 0.0, -1e30)
    c["c_fc"] = fc
    vci = np.zeros((128, 2, 97), np.float32)
    vci[:, :, 64] = 1.0
    cp = np.arange(128)[:, None]
    c0 = (cp - 1) * 16
    j0 = np.arange(32)[None, :] * 64
    ov = (c0 < j0 + 64) & (c0 + 32 > j0) & (cp >= 1)
    vci[:, 0, 65:97] = ov
    vci[:, 1, 65:97] = ov
    c["c_vcinit"] = vci
    half = 32
    invf = (np.float32(10000.0) ** (-np.arange(half, dtype=np.float32) / np.float32(half))).astype(np.float32)
    c["c_invf"] = np.ascontiguousarray(np.broadcast_to(invf[None, :], (128, 32))).astype(np.float32)
    return c


def _prepare_inputs(x, c, positions, w_ada, b_ada, w_in, cmp_pos_k, cmp_w1_k, cmp_w2_k,
                    cmp_pos_v, cmp_w1_v, cmp_w2_v, sinks, w_up_a, w_up_b, w_out, ln_g, ln_b):
    f32 = lambda a: np.ascontiguousarray(np.asarray(a, dtype=np.float32))
    x = f32(x); c = f32(c); positions = np.asarray(positions).astype(np.int32)
    w_ada = f32(w_ada)[0]; b_ada = f32(b_ada)[0]; w_in = f32(w_in)[0]
    offs = np.cumsum([0] + _IN_SPLITS)
    col = lambda i: w_in[:, offs[i]:offs[i + 1]]
    q_a, kv_cmp, kv_sel, kv_win, g_nsa, z_a, q_b, kv_b, z_b, g_merge = [col(i) for i in range(10)]
    qperm = lambda t: t.reshape(D, 2, 4, 64).transpose(0, 2, 1, 3).reshape(D, 512)
    q_a = qperm(q_a); q_b = qperm(q_b)
    kpart = lambda t: t[:, :128]
    vpart = lambda t: t[:, 128:]
    w_inP = np.concatenate([q_a, q_b, kpart(kv_cmp), kpart(kv_sel), kpart(kv_win), kpart(kv_b),
                            vpart(kv_cmp), vpart(kv_sel), vpart(kv_win), vpart(kv_b), g_nsa], axis=1)
    w_inQ = np.concatenate([z_a, z_b, g_merge], axis=1)
    assert w_inP.shape[1] == PW and w_inQ.shape[1] == QW

    def w1_layout(w1):
        return np.ascontiguousarray(f32(w1)[0].reshape(16, 2, 64, 256).transpose(1, 2, 0, 3).reshape(128, 16, 256))

    def pos_layout(p):
        return np.ascontiguousarray(f32(p)[0].reshape(16, 2, 64).transpose(1, 2, 0).reshape(128, 16))
    w2k = f32(cmp_w2_k)[0].reshape(2, 128, 64).transpose(1, 0, 2)
    w2k = np.ascontiguousarray(np.concatenate([w2k, w2k], axis=2))
    w2v = np.ascontiguousarray(f32(cmp_w2_v)[0].reshape(2, 128, 64).transpose(1, 0, 2))
    shared = {
        "w_ada": w_ada, "b_adaT": np.ascontiguousarray(b_ada.reshape(24, 128).T),
        "b_gate": np.ascontiguousarray(b_ada[None, 2048:]),
        "w_inP": np.ascontiguousarray(w_inP), "w_inQ": np.ascontiguousarray(w_inQ),
        "w1k": w1_layout(cmp_w1_k), "w1v": w1_layout(cmp_w1_v),
        "posk": pos_layout(cmp_pos_k), "posv": pos_layout(cmp_pos_v),
        "w2k": w2k, "w2v": w2v, "sinks": f32(sinks),
        "w_up_a": f32(w_up_a)[0], "w_up_b": f32(w_up_b)[0], "w_out": f32(w_out)[0],
        "ln_g": f32(ln_g), "ln_b": f32(ln_b),
    }
    shared.update(_constants())
    in_maps = []
    for i in range(8):
        m = dict(shared)
        m["x"] = np.ascontiguousarray(x[NB * i:NB * (i + 1)].reshape(NB * S, D))
        cc = c[NB * i:NB * (i + 1)]
        m["cT"] = np.ascontiguousarray(cc.T.reshape(8, 128, NB).transpose(1, 0, 2))
        pp = positions[NB * i:NB * (i + 1)]
        m["posT"] = np.ascontiguousarray(pp.reshape(NB, NT, 128).transpose(2, 0, 1))
        in_maps.append(m)
    return in_maps


_PROGRAM = {}


def kernel(**inputs):
    in_maps = _prepare_inputs(**inputs)
    if "nc" not in _PROGRAM:
        _PROGRAM["nc"] = build_program()[0]
    res = run_bass_kernel_spmd(_PROGRAM["nc"], in_maps, core_ids=list(range(8)))
    outs = [np.asarray(r["out"]).reshape(NB, S, D) for r in res.results]
    return np.concatenate(outs, axis=0).astype(np.float32)
```
